# Optimizing a Trainium2 kernel written in Bass

```python
import math
import jax, jax.numpy as jnp
from jax import lax
import numpy as np

D_MODEL = 1024
BATCH = 16
SEQ = 2048
DEPTH = 1

DA_HEADS = D_MODEL // 128
DA_HEAD_DIM = 64
SWA_Q_HEADS = D_MODEL // 64
SWA_KV_HEADS = SWA_Q_HEADS // 4
SWA_HEAD_DIM = 64
WINDOW = 128
Q_BLOCK = 128
D_FF = ((8 * D_MODEL // 3 + 127) // 128) * 128
RMS_EPS = 1e-6

DA_QK_W = DA_HEADS * 2 * DA_HEAD_DIM
DA_V_W = DA_HEADS * 2 * DA_HEAD_DIM
SWA_Q_W = SWA_Q_HEADS * SWA_HEAD_DIM
SWA_KV_W = SWA_KV_HEADS * SWA_HEAD_DIM
IN_COLS = 2 * DA_QK_W + DA_V_W + SWA_Q_W + 2 * SWA_KV_W + 2 * D_MODEL

kernel_name = "hybrid_diffattn_swa_gated_macaron"


def rmsnorm(x, g):
    xf = x.astype(jnp.float32)
    y = xf * lax.rsqrt(jnp.mean(xf * xf, axis=-1, keepdims=True) + RMS_EPS)
    return (y * g.astype(jnp.float32)).astype(x.dtype)


def alibi_slopes(n):
    return 2.0 ** (-8.0 * jnp.arange(1, n + 1, dtype=jnp.float32) / n)


def swiglu(x, w_gate, w_up, w_down):
    return (jax.nn.silu(x @ w_gate) * (x @ w_up)) @ w_down


def diff_attention(q, k, v, lam, lam_init, subnorm_g):
    B, S = q.shape[0], q.shape[1]
    nb = S // Q_BLOCK
    scale = DA_HEAD_DIM ** -0.5
    slopes = alibi_slopes(DA_HEADS)
    kpos = jnp.arange(S)
    qb = q.reshape(B, nb, Q_BLOCK, DA_HEADS, 2, DA_HEAD_DIM).transpose(1, 0, 2, 3, 4, 5)

    def block(args):
        qblk, n = args
        s = jnp.einsum('bqhcd,bkhcd->bhcqk', qblk, k).astype(jnp.float32) * scale
        qpos = n * Q_BLOCK + jnp.arange(Q_BLOCK)
        dist = jnp.abs(qpos[:, None] - kpos[None, :]).astype(jnp.float32)
        s = s - slopes[:, None, None, None] * dist
        p = jax.nn.softmax(s, axis=-1)
        a = p[:, :, 0] - lam * p[:, :, 1]
        return jnp.einsum('bhqk,bkhe->bqhe', a.astype(v.dtype), v)

    o = lax.map(block, (qb, jnp.arange(nb)))
    o = o.transpose(1, 0, 2, 3, 4).reshape(B, S, DA_HEADS, 2 * DA_HEAD_DIM)
    o = rmsnorm(o, subnorm_g) * (1.0 - lam_init)
    return o.reshape(B, S, DA_V_W)


def window_attention(q, k, v, sink):
    B, S = q.shape[0], q.shape[1]
    nb = S // Q_BLOCK
    G = SWA_Q_HEADS // SWA_KV_HEADS
    scale = SWA_HEAD_DIM ** -0.5
    qb = q.reshape(B, nb, Q_BLOCK, SWA_KV_HEADS, G, SWA_HEAD_DIM)

    def band(t):
        tp = jnp.pad(t, ((0, 0), (Q_BLOCK, Q_BLOCK), (0, 0), (0, 0)))
        tp = tp.reshape(B, nb + 2, Q_BLOCK, SWA_KV_HEADS, SWA_HEAD_DIM)
        return jnp.concatenate([tp[:, :-2], tp[:, 1:-1], tp[:, 2:]], axis=2)

    kw, vw = band(k), band(v)
    s = jnp.einsum('bnqhgd,bnkhd->bnhgqk', qb, kw).astype(jnp.float32) * scale
    r = jnp.arange(Q_BLOCK)
    j = jnp.arange(3 * Q_BLOCK)
    dist = jnp.abs(r[:, None] - j[None, :] + Q_BLOCK)
    kpos = jnp.arange(nb)[:, None] * Q_BLOCK - Q_BLOCK + j[None, :]
    valid = (dist <= WINDOW)[None] & ((kpos >= 0) & (kpos < S))[:, None, :]
    slopes = alibi_slopes(SWA_Q_HEADS).reshape(SWA_KV_HEADS, G)
    s = s - slopes[:, :, None, None] * dist.astype(jnp.float32)
    s = jnp.where(valid[:, None, None], s, -jnp.inf)
    sink_l = sink.astype(jnp.float32).reshape(SWA_KV_HEADS, G)[:, :, None, None]
    m = jnp.maximum(jnp.max(s, axis=-1, keepdims=True), sink_l)
    e = jnp.exp(s - m)
    p = e / (jnp.sum(e, axis=-1, keepdims=True) + jnp.exp(sink_l - m))
    o = jnp.einsum('bnhgqk,bnkhd->bnqhgd', p.astype(v.dtype), vw)
    return o.reshape(B, S, SWA_Q_W)


def setup_inputs(seed: int = 0) -> dict:
    key = jax.random.key(seed)
    ks = jax.random.split(key, 24)
    f32 = jnp.float32

    def w(k, shape, fan_in, mult=1.0):
        return jax.random.normal(k, shape, f32) * (mult * fan_in ** -0.5)

    def gain(k, shape):
        return 1.0 + 0.02 * jax.random.normal(k, shape, f32)

    return {
        "x": jax.random.normal(ks[0], (BATCH, SEQ, D_MODEL), f32),
        "norm_ffn1": gain(ks[1], (DEPTH, D_MODEL)),
        "ffn1_gate": w(ks[2], (DEPTH, D_MODEL, D_FF), D_MODEL),
        "ffn1_up": w(ks[3], (DEPTH, D_MODEL, D_FF), D_MODEL),
        "ffn1_down": w(ks[4], (DEPTH, D_FF, D_MODEL), D_FF),
        "norm_mix": gain(ks[5], (DEPTH, D_MODEL)),
        "w_in": w(ks[6], (DEPTH, D_MODEL, IN_COLS), D_MODEL),
        "b_gate": 0.01 * jax.random.normal(ks[7], (DEPTH, 2, D_MODEL), f32),
        "da_lambda": 0.1 * jax.random.normal(ks[8], (DEPTH, 4, DA_HEAD_DIM), f32),
        "da_subnorm": gain(ks[9], (DEPTH, 2 * DA_HEAD_DIM)),
        "swa_sink": 0.5 * jax.random.normal(ks[10], (DEPTH, SWA_Q_HEADS), f32),
        "w_proj_da": w(ks[11], (DEPTH, DA_V_W, D_MODEL), DA_V_W),
        "w_proj_swa": w(ks[12], (DEPTH, SWA_Q_W, D_MODEL), SWA_Q_W),
        "w_out": w(ks[13], (DEPTH, D_MODEL, D_MODEL), D_MODEL),
        "norm_ffn2": gain(ks[14], (DEPTH, D_MODEL)),
        "ffn2_gate": w(ks[15], (DEPTH, D_MODEL, D_FF), D_MODEL),
        "ffn2_up": w(ks[16], (DEPTH, D_MODEL, D_FF), D_MODEL),
        "ffn2_down": w(ks[17], (DEPTH, D_FF, D_MODEL), D_FF),
        "norm_final": gain(ks[18], (D_MODEL,)),
    }


def reference(x, norm_ffn1, ffn1_gate, ffn1_up, ffn1_down, norm_mix, w_in, b_gate,
              da_lambda, da_subnorm, swa_sink, w_proj_da, w_proj_swa, w_out,
              norm_ffn2, ffn2_gate, ffn2_up, ffn2_down, norm_final):
    B, S, _ = x.shape
    splits = np.cumsum([DA_QK_W, DA_QK_W, DA_V_W, SWA_Q_W, SWA_KV_W, SWA_KV_W]).tolist()
    for l in range(DEPTH):
        x = x + 0.5 * swiglu(rmsnorm(x, norm_ffn1[l]), ffn1_gate[l], ffn1_up[l], ffn1_down[l])

        h = rmsnorm(x, norm_mix[l])
        proj = h @ w_in[l]
        da_q, da_k, da_v, sw_q, sw_k, sw_v, gate = jnp.split(proj, splits, axis=-1)

        lam_init = 0.8 - 0.6 * math.exp(-0.3 * l)
        lp = da_lambda[l].astype(jnp.float32)
        lam = jnp.exp(jnp.sum(lp[0] * lp[1])) - jnp.exp(jnp.sum(lp[2] * lp[3])) + lam_init
        o_da = diff_attention(
            da_q.reshape(B, S, DA_HEADS, 2, DA_HEAD_DIM),
            da_k.reshape(B, S, DA_HEADS, 2, DA_HEAD_DIM),
            da_v.reshape(B, S, DA_HEADS, 2 * DA_HEAD_DIM),
            lam, lam_init, da_subnorm[l])
        o_sw = window_attention(
            sw_q.reshape(B, S, SWA_Q_HEADS, SWA_HEAD_DIM),
            sw_k.reshape(B, S, SWA_KV_HEADS, SWA_HEAD_DIM),
            sw_v.reshape(B, S, SWA_KV_HEADS, SWA_HEAD_DIM),
            swa_sink[l])

        g = jax.nn.sigmoid(gate.reshape(B, S, 2, D_MODEL) + b_gate[l])
        merged = g[:, :, 0] * (o_da @ w_proj_da[l]) + g[:, :, 1] * (o_sw @ w_proj_swa[l])
        x = x + merged @ w_out[l]

        x = x + 0.5 * swiglu(rmsnorm(x, norm_ffn2[l]), ffn2_gate[l], ffn2_up[l], ffn2_down[l])
    return rmsnorm(x, norm_final)
```

```python
import numpy as np
from contextlib import ExitStack
import concourse.bass as bass
import concourse.mybir as mybir
from concourse.bass_utils import run_bass_kernel_spmd

F32 = mybir.dt.float32
BF16 = mybir.dt.bfloat16
AF = mybir.ActivationFunctionType
ALU = mybir.AluOpType
AX = mybir.AxisListType

NCORES = 8
T = 4096
SEQ = 2048
D = 1024
DFF = 2816
NFF = 22
INC = 6656
TB = 1024
NT = TB // 128


class Sched:
    ENG = ("pe", "act", "dve", "pool", "sp")

    def __init__(self, nc):
        self.nc = nc
        self.eng = {"pe": nc.tensor, "act": nc.scalar, "dve": nc.vector,
                    "pool": nc.gpsimd, "sp": nc.sync}
        self.ops = []
        self.last_w = {}
        self.readers = {}
        self.sems = {}
        self.counts = {}
        self.sig = {}
        self.waited = {e: {} for e in self.ENG}
        self.emitted = 0
        self.last_on = {}

    def add(self, eng, fn, r=(), w=(), dma=None, barrier=False):
        idx = len(self.ops)
        deps = {}
        for k in r:
            lw = self.last_w.get(k)
            if lw is not None:
                deps[lw] = True
        for k in w:
            lw = self.last_w.get(k)
            if lw is not None:
                if k in ("junk", "jk2"):
                    deps[lw] = True
                deps.setdefault(lw, False)
            for rd in self.readers.get(k, ()):
                deps.setdefault(rd, False)
        for k in r:
            self.readers.setdefault(k, []).append(idx)
        for k in w:
            self.last_w[k] = idx
            self.readers[k] = []
        deps.pop(idx, None)
        self.ops.append([eng, fn, deps, dma, barrier])
        if dma is None:
            self.last_on[eng] = idx
        return idx

    def barrier(self):
        lasts = dict(self.last_on)
        for e in self.ENG:
            idx = self.add(e, lambda en: en.nop(), barrier=True)
            for e2, li in lasts.items():
                if li >= self.emitted:
                    self.ops[idx][2][li] = True
        self.last_w = {}
        self.readers = {}
        self.emit()

    def _getsem(self, key):
        if key not in self.sems:
            self.sems[key] = self.nc.alloc_semaphore(name="s%d" % len(self.sems))
            self.counts[key] = 0
        return self.sems[key]

    def emit(self):
        ops = self.ops
        n = len(ops)
        start = self.emitted
        need = {}
        for i in range(start, n):
            eng, fn, deps, dma, bar = ops[i]
            for d, israw in deps.items():
                deng, _, _, ddma, _ = ops[d]
                if ddma is not None:
                    continue
                if deng != eng or (israw and eng != "pe"):
                    need[d] = True
        for i in range(start, n):
            eng, fn, deps, dma, bar = ops[i]
            e = self.eng[eng]
            wl = {}
            for d, israw in deps.items():
                deng, _, _, ddma, _ = ops[d]
                if ddma is not None:
                    key = ("d", ddma)
                    val = self.counts[key]
                elif deng != eng or (israw and eng != "pe"):
                    key, val = self.sig[d]
                else:
                    continue
                if wl.get(key, 0) < val:
                    wl[key] = val
            if bar:
                for key, val in self.counts.items():
                    if key[0] == "d" and val > 0:
                        wl[key] = val
            for key, val in wl.items():
                if self.waited[eng].get(key, 0) >= val:
                    continue
                self.waited[eng][key] = val
                e.wait_ge(self.sems[key], val)
            ins = fn(e)
            if dma is not None:
                key = ("d", dma)
                s = self._getsem(key)
                self.counts[key] += 16
                ins.then_inc(s, 16)
                self.sig[i] = (key, self.counts[key])
            elif need.get(i):
                key = ("e", eng)
                s = self._getsem(key)
                self.counts[key] += 1
                ins.then_inc(s, 1)
                self.sig[i] = (key, self.counts[key])
        self.emitted = n

    def finish(self):
        self.barrier()


def build_program(debug=False, phases=("A", "B", "C"), nseq=2):
    nc = bass.Bass("TRN2", target_bir_lowering=False)

    def din(name, shape, dt=F32):
        return nc.dram_tensor(name, shape, dt, kind="ExternalInput").ap()

    skind = "ExternalOutput" if debug else "Internal"

    def dscr(name, shape, dt):
        return nc.dram_tensor(name, shape, dt, kind=skind).ap()

    x_in = din("x", [T, D])
    w_names = {}
    for nm, shp in [("norm_ffn1", [D]), ("ffn1_gate", [D, DFF]), ("ffn1_up", [D, DFF]), ("ffn1_down", [DFF, D]),
                    ("norm_mix", [D]), ("w_in", [D, INC]), ("b_gate", [2 * D]), ("da_lambda", [256]),
                    ("da_subnorm", [128]), ("swa_sink", [16]), ("w_proj_da", [D, D]), ("w_proj_swa", [D, D]),
                    ("w_out", [D, D]), ("norm_ffn2", [D]), ("ffn2_gate", [D, DFF]), ("ffn2_up", [D, DFF]),
                    ("ffn2_down", [DFF, D]), ("norm_final", [D])]:
        w_names[nm] = din(nm, shp)
    W = w_names
    out = nc.dram_tensor("out", [T, D], F32, kind="ExternalOutput").ap()

    x1s = dscr("x1s", [T, D], F32)
    qTd = dscr("qTd", [D, T], BF16)
    kTd = dscr("kTd", [D, T], BF16)
    vd = dscr("vd", [T, D], BF16)
    qTs = dscr("qTs", [D, T], BF16)
    kTs = dscr("kTs", [256, T], BF16)
    vs = dscr("vs", [T, 256], BF16)
    gTs = dscr("gTs", [2 * D, T], F32)
    oTd = dscr("oTd", [D, T], BF16)
    oTs = dscr("oTs", [D, T], BF16)

    S = Sched(nc)

    uid = [0]

    def sb(name, shape, dt):
        uid[0] += 1
        return nc.sbuf_tensor("%s_%d" % (name, uid[0]), shape, dt)

    def pst(name, shape, dt):
        uid[0] += 1
        return nc.psum_tensor("%s_%d" % (name, uid[0]), shape, dt)

    ident = nc.alloc_sbuf_tensor("ident", [128, 128], BF16).ap()
    onesb = nc.alloc_sbuf_tensor("onesb", [128, 128], BF16).ap()
    epsb = nc.alloc_sbuf_tensor("epsb", [128, 1], F32).ap()
    gb = nc.alloc_sbuf_tensor("gb", [128, 4, D], F32).ap()
    bgT = nc.alloc_sbuf_tensor("bgT", [128, 16], F32).ap()
    lt = nc.alloc_sbuf_tensor("lt", [128, 256], F32).ap()
    ltmp = nc.alloc_sbuf_tensor("ltmp", [128, 64], F32).ap()
    s12 = nc.alloc_sbuf_tensor("s12", [128, 2], F32).ap()
    e12 = nc.alloc_sbuf_tensor("e12", [128, 2], F32).ap()
    nl0 = nc.alloc_sbuf_tensor("nl0", [128, 1], F32).ap()
    neglam = nc.alloc_sbuf_tensor("neglam", [128, 1], F32).ap()
    gsub0 = nc.alloc_sbuf_tensor("gsub0", [128, 128], F32).ap()
    gsub = nc.alloc_sbuf_tensor("gsub", [128, 128], F32).ap()
    sk0 = nc.alloc_sbuf_tensor("sk0", [128, 16], F32).ap()
    es = nc.alloc_sbuf_tensor("es", [128, 16], F32).ap()
    Tsw = nc.alloc_sbuf_tensor("Tsw", [128, 16, 384], BF16).ap()

    def setup():
        S.add("pool", lambda e: e.memset(onesb, 1.0), w=["onesb"])
        S.add("pool", lambda e: e.affine_select(out=ident, in_=onesb, pattern=[[1, 128]], compare_op=ALU.is_equal,
                                               fill=0.0, base=0, channel_multiplier=-1), r=["onesb"], w=["ident"])
        S.add("pool", lambda e: e.memset(epsb, 1e-6), w=["epsb"])
        for i, nm in enumerate(["norm_ffn1", "norm_mix", "norm_ffn2", "norm_final"]):
            S.add("sp", lambda e, i=i, nm=nm: e.dma_start(out=gb[:, i, :], in_=W[nm].partition_broadcast(128)),
                  w=[("gb", i)], dma="setup")
        S.add("sp", lambda e: e.dma_start(out=bgT, in_=W["b_gate"].rearrange("(c p) -> p c", p=128),
                                         allow_slow_non_contiguous=True), w=["bgT"], dma="setup")
        S.add("sp", lambda e: e.dma_start(out=lt, in_=W["da_lambda"].partition_broadcast(128)), w=["lt"], dma="setup")
        S.add("sp", lambda e: e.dma_start(out=gsub0, in_=W["da_subnorm"].partition_broadcast(128)), w=["gsub0"], dma="setup")
        S.add("sp", lambda e: e.dma_start(out=sk0, in_=W["swa_sink"].partition_broadcast(128)), w=["sk0"], dma="setup")
        S.add("dve", lambda e: e.tensor_tensor(out=ltmp, in0=lt[:, 0:64], in1=lt[:, 64:128], op=ALU.mult), r=["lt"], w=["ltmp"])
        S.add("dve", lambda e: e.reduce_sum(out=s12[:, 0:1], in_=ltmp, axis=AX.X), r=["ltmp"], w=["s12a"])
        S.add("dve", lambda e: e.tensor_tensor(out=ltmp, in0=lt[:, 128:192], in1=lt[:, 192:256], op=ALU.mult), r=["lt", "s12a"], w=["ltmp"])
        S.add("dve", lambda e: e.reduce_sum(out=s12[:, 1:2], in_=ltmp, axis=AX.X), r=["ltmp"], w=["s12b"])
        S.add("act", lambda e: e.activation(out=e12, in_=s12, func=AF.Exp), r=["s12a", "s12b"], w=["e12"])
        S.add("dve", lambda e: e.tensor_tensor(out=nl0, in0=e12[:, 1:2], in1=e12[:, 0:1], op=ALU.subtract), r=["e12"], w=["nl0"])
        S.add("dve", lambda e: e.tensor_scalar(out=neglam, in0=nl0, scalar1=-0.2, scalar2=None, op0=ALU.add), r=["nl0"], w=["neglam"])
        S.add("dve", lambda e: e.tensor_scalar(out=gsub, in0=gsub0, scalar1=0.8, scalar2=None, op0=ALU.mult), r=["gsub0"], w=["gsub"])
        S.add("act", lambda e: e.activation(out=es, in_=sk0, func=AF.Exp), r=["sk0"], w=["es"])
        with ExitStack() as _st1:
            dswi_t = _st1.enter_context(sb("dswi", [128, 384], F32))
            dswa_t = _st1.enter_context(sb("dswa", [128, 384], F32))
            dswb_t = _st1.enter_context(sb("dswb", [128, 384], F32))
            dswi, dswa, dswb = dswi_t.ap(), dswa_t.ap(), dswb_t.ap()
            S.add("pool", lambda e: e.iota(dswi, [[1, 384]], base=-128, channel_multiplier=-1,
                                           allow_small_or_imprecise_dtypes=True), w=["dswi"])
            S.add("act", lambda e: e.activation(out=dswa, in_=dswi, func=AF.Abs), r=["dswi"], w=["dswa"])
            S.add("pool", lambda e: e.affine_select(out=dswb, in_=dswa, pattern=[[1, 384]], compare_op=ALU.is_ge,
                                                   fill=1.0e6, base=0, channel_multiplier=-1), r=["dswa"], w=["dswb"])
            S.add("pool", lambda e: e.affine_select(out=dswi, in_=dswb, pattern=[[-1, 384]], compare_op=ALU.is_ge,
                                                   fill=1.0e6, base=256, channel_multiplier=1), r=["dswb"], w=["dswi2"])
            for h in range(16):
                sl = 2.0 ** (-8.0 * (h + 1) / 16.0)
                S.add("act", lambda e, h=h, sl=sl: e.activation(out=Tsw[:, h, :], in_=dswi, func=AF.Exp, scale=-sl),
                      r=["dswi2"], w=[("Tsw", h)])
            S.barrier()

    def norm_to_hT(xres, gi, hT, hb, ptr, junk, ssq, lnv, rstd, store=None):
        def stats(t):
            S.add("act", lambda e: e.activation(out=junk, in_=xres[:, t, :], func=AF.Square, scale=1.0 / 32.0,
                                                accum_out=ssq[:, t:t + 1]),
                  r=[("x", t)], w=["junk", ("ssq", t)])
            S.add("act", lambda e: e.activation(out=lnv[:, t:t + 1], in_=ssq[:, t:t + 1], func=AF.Ln, bias=epsb, scale=1.0),
                  r=[("ssq", t)], w=[("lnv", t)])
            S.add("act", lambda e: e.activation(out=rstd[:, t:t + 1], in_=lnv[:, t:t + 1], func=AF.Exp, scale=-0.5),
                  r=[("lnv", t)], w=[("rstd", t)])
        stats(0)
        stats(1)
        for t in range(NT):
            if t + 2 < NT:
                stats(t + 2)
            hbt = hb[t % 2]
            pt = ptr[t % 2]
            S.add("dve", lambda e, t=t, hbt=hbt: e.scalar_tensor_tensor(out=hbt, in0=xres[:, t, :], scalar=rstd[:, t:t + 1],
                                                                     in1=gb[:, gi, :], op0=ALU.mult, op1=ALU.mult),
                  r=[("x", t), ("rstd", t), ("gb", gi)], w=[("hb", t % 2)])
            for kc in range(8):
                S.add("pe", lambda e, kc=kc, hbt=hbt, pt=pt: e.transpose(out=pt[:, kc * 128:(kc + 1) * 128],
                                                                          in_=hbt[:, kc * 128:(kc + 1) * 128], identity=ident),
                      r=[("hb", t % 2), "ident"], w=[("ptr", t % 2)])
            S.add("act", lambda e, t=t, pt=pt: e.activation(out=hT[:, :, t * 128:(t + 1) * 128],
                                                           in_=pt.rearrange("p (k t) -> p k t", k=8), func=AF.Copy),
                  r=[("ptr", t % 2)], w=[("hT", t)])
            if store is not None:
                store(t)

    def ffn(xres, hT, wg_d, wu_d, wd_d, pg, pu, py):
        with ExitStack() as _st2:
            aT_t = _st2.enter_context(sb("aT", [128, NFF, TB], BF16))
            wd_t = _st2.enter_context(sb("wdb", [128, NFF, D], BF16))
            wg0 = _st2.enter_context(sb("wg0", [128, 8, 256], BF16))
            wg1 = _st2.enter_context(sb("wg1", [128, 8, 256], BF16))
            wu0 = _st2.enter_context(sb("wu0", [128, 8, 256], BF16))
            wu1 = _st2.enter_context(sb("wu1", [128, 8, 256], BF16))
            sg0 = _st2.enter_context(sb("sg0", [128, 512], F32))
            sg1 = _st2.enter_context(sb("sg1", [128, 512], F32))
            aT, wdb = aT_t.ap(), wd_t.ap()
            wg = [wg0.ap(), wg1.ap()]
            wu = [wu0.ap(), wu1.ap()]
            sg = [sg0.ap(), sg1.ap()]
            NG = 11

            def load_gu(g):
                s = g % 2
                S.add("pool", lambda e: e.dma_start(out=wg[s], in_=wg_d[:, g * 256:(g + 1) * 256].rearrange("(kc p) f -> p kc f", p=128)),
                      w=[("wg", s)], dma="wg%d" % s)
                S.add("pool", lambda e: e.dma_start(out=wu[s], in_=wu_d[:, g * 256:(g + 1) * 256].rearrange("(kc p) f -> p kc f", p=128)),
                      w=[("wu", s)], dma="wu%d" % s)

            def load_wd(i):
                S.add("pool", lambda e: e.dma_start(out=wdb[:, 2 * i:2 * i + 2, :],
                                                   in_=wd_d[i * 256:(i + 1) * 256, :].rearrange("(c p) f -> p c f", p=128)),
                      w=[("wd", i)], dma="wd")

            load_gu(0)
            load_gu(1)
            cnt = 0
            for g in range(NG):
                s = g % 2
                for c2 in range(2):
                    ffc = g * 2 + c2
                    for sub in range(TB // 512):
                        par = cnt % 2
                        cnt += 1
                        tk = [("hT", t) for t in range(sub * 4, sub * 4 + 4)]
                        for kc in range(8):
                            S.add("pe", lambda e, kc=kc, s=s, c2=c2, sub=sub, par=par: e.matmul(
                                pg[par], lhsT=wg[s][:, kc, c2 * 128:(c2 + 1) * 128], rhs=hT[:, kc, sub * 512:(sub + 1) * 512],
                                start=(kc == 0), stop=(kc == 7)), r=[("wg", s)] + tk, w=[("pg", par)])
                        for kc in range(8):
                            S.add("pe", lambda e, kc=kc, s=s, c2=c2, sub=sub, par=par: e.matmul(
                                pu[par], lhsT=wu[s][:, kc, c2 * 128:(c2 + 1) * 128], rhs=hT[:, kc, sub * 512:(sub + 1) * 512],
                                start=(kc == 0), stop=(kc == 7)), r=[("wu", s)] + tk, w=[("pu", par)])
                        S.add("act", lambda e, par=par: e.activation(out=sg[par], in_=pg[par], func=AF.Silu),
                              r=[("pg", par)], w=[("sg", par)])
                        S.add("dve", lambda e, par=par, ffc=ffc, sub=sub: e.tensor_tensor(
                            out=aT[:, ffc, sub * 512:(sub + 1) * 512], in0=sg[par], in1=pu[par], op=ALU.mult),
                            r=[("sg", par), ("pu", par)], w=[("aT", ffc, sub)])
                if g + 2 < NG:
                    load_gu(g + 2)
                load_wd(g)
            cnt = 0
            for t in range(NT):
                for half in range(2):
                    par = cnt % 2
                    cnt += 1
                    for ffc in range(NFF):
                        S.add("pe", lambda e, t=t, half=half, ffc=ffc, par=par: e.matmul(
                            py[par], lhsT=aT[:, ffc, t * 128:(t + 1) * 128], rhs=wdb[:, ffc, half * 512:(half + 1) * 512],
                            start=(ffc == 0), stop=(ffc == NFF - 1)),
                            r=[("aT", ffc, t // 4), ("wd", ffc // 2)], w=[("py", par)])
                    S.add("dve", lambda e, t=t, half=half, par=par: e.scalar_tensor_tensor(
                        out=xres[:, t, half * 512:(half + 1) * 512], in0=py[par], scalar=0.5,
                        in1=xres[:, t, half * 512:(half + 1) * 512], op0=ALU.mult, op1=ALU.add),
                        r=[("py", par), ("x", t)], w=[("x", t)])

    def phase_A(blk):
        r0 = blk * TB
        with ExitStack() as _st3:
            xres_t = _st3.enter_context(sb("xres", [128, NT, D], F32))
            hT_t = _st3.enter_context(sb("hT", [128, 8, TB], BF16))
            hb0 = _st3.enter_context(sb("hb0", [128, D], BF16))
            hb1 = _st3.enter_context(sb("hb1", [128, D], BF16))
            junk_t = _st3.enter_context(sb("junk", [128, D], BF16))
            ssq_t = _st3.enter_context(sb("ssq", [128, NT], F32))
            lnv_t = _st3.enter_context(sb("lnv", [128, NT], F32))
            rstd_t = _st3.enter_context(sb("rstd", [128, NT], F32))
            xres, hT = xres_t.ap(), hT_t.ap()
            hb = [hb0.ap(), hb1.ap()]
            junk, ssq, lnv, rstd = junk_t.ap(), ssq_t.ap(), lnv_t.ap(), rstd_t.ap()
            with ExitStack() as _st4:
                p0 = _st4.enter_context(pst("ptr0", [128, 1024], BF16))
                p1 = _st4.enter_context(pst("ptr1", [128, 1024], BF16))
                pg0 = _st4.enter_context(pst("pg0", [128, 512], F32))
                pg1 = _st4.enter_context(pst("pg1", [128, 512], F32))
                pu0 = _st4.enter_context(pst("pu0", [128, 512], F32))
                pu1 = _st4.enter_context(pst("pu1", [128, 512], F32))
                py0 = _st4.enter_context(pst("py0", [128, 512], F32))
                py1 = _st4.enter_context(pst("py1", [128, 512], F32))
                ptr = [p0.ap(), p1.ap()]
                pg = [pg0.ap(), pg1.ap()]
                pu = [pu0.ap(), pu1.ap()]
                py = [py0.ap(), py1.ap()]
                for t in range(NT):
                    S.add("sp", lambda e, t=t: e.dma_start(out=xres[:, t, :], in_=x_in[r0 + t * 128:r0 + (t + 1) * 128, :]),
                          w=[("x", t)], dma="x%d" % t)
                norm_to_hT(xres, 0, hT, hb, ptr, junk, ssq, lnv, rstd)
                ffn(xres, hT, W["ffn1_gate"], W["ffn1_up"], W["ffn1_down"], pg, pu, py)

                def store_x1(t):
                    S.add("sp", lambda e, t=t: e.dma_start(out=x1s[r0 + t * 128:r0 + (t + 1) * 128, :], in_=xres[:, t, :]),
                          r=[("x", t)], dma="st")
                norm_to_hT(xres, 1, hT, hb, ptr, junk, ssq, lnv, rstd, store=store_x1)
                S.barrier()
            with ExitStack() as _st5:
                wi0 = _st5.enter_context(sb("wi0", [128, 8, 256], BF16))
                wi1 = _st5.enter_context(sb("wi1", [128, 8, 256], BF16))
                wi2 = _st5.enter_context(sb("wi2", [128, 8, 256], BF16))
                sgb0 = _st5.enter_context(sb("sgb0", [128, TB], BF16))
                sgb1 = _st5.enter_context(sb("sgb1", [128, TB], BF16))
                sgf0 = _st5.enter_context(sb("sgf0", [128, TB], F32))
                sgf1 = _st5.enter_context(sb("sgf1", [128, TB], F32))
                svb0 = _st5.enter_context(sb("svb0", [128, NT, 256], BF16))
                svb1 = _st5.enter_context(sb("svb1", [128, NT, 256], BF16))
                pq0 = _st5.enter_context(pst("pq0", [128, 512], F32))
                pq1 = _st5.enter_context(pst("pq1", [128, 512], F32))
                pq2 = _st5.enter_context(pst("pq2", [128, 512], F32))
                pq3 = _st5.enter_context(pst("pq3", [128, 512], F32))
                wi = [wi0.ap(), wi1.ap(), wi2.ap()]
                sgb = [sgb0.ap(), sgb1.ap()]
                sgf = [sgf0.ap(), sgf1.ap()]
                svb = [svb0.ap(), svb1.ap()]
                pq = [pq0.ap(), pq1.ap(), pq2.ap(), pq3.ap()]
                NG = 26
                allh = [("hT", t) for t in range(NT)]

                def load_wi(g):
                    s = g % 3
                    S.add("pool", lambda e: e.dma_start(out=wi[s], in_=W["w_in"][:, g * 256:(g + 1) * 256].rearrange("(kc p) f -> p kc f", p=128)),
                          w=[("wi", s)], dma="wi%d" % s)
                load_wi(0)
                load_wi(1)
                load_wi(2)
                pcnt = 0
                ccnt = 0
                vcnt = 0
                for g in range(NG):
                    s = g % 3
                    col0 = g * 256
                    if (8 <= g < 12) or g == 17:
                        sv = svb[vcnt % 2]
                        svk = ("svb", vcnt % 2)
                        vcnt += 1
                        for t in range(NT):
                            pp = pq[pcnt % 4]
                            ppk = ("pq", pcnt % 4)
                            pcnt += 1
                            for kc in range(8):
                                S.add("pe", lambda e, kc=kc, t=t, s=s, pp=pp: e.matmul(
                                    pp[:, 0:256], lhsT=hT[:, kc, t * 128:(t + 1) * 128], rhs=wi[s][:, kc, :],
                                    start=(kc == 0), stop=(kc == 7)), r=[("wi", s), ("hT", t)], w=[ppk])
                            eng = "dve" if t % 2 == 0 else "act"
                            if eng == "dve":
                                S.add("dve", lambda e, t=t, pp=pp, sv=sv: e.tensor_copy(out=sv[:, t, :], in_=pp[:, 0:256]),
                                      r=[ppk], w=[(svk, t)])
                            else:
                                S.add("act", lambda e, t=t, pp=pp, sv=sv: e.activation(out=sv[:, t, :], in_=pp[:, 0:256], func=AF.Copy),
                                      r=[ppk], w=[(svk, t)])
                        if g == 17:
                            dst = vs[r0:r0 + TB, :].rearrange("(t p) f -> p t f", p=128)
                        else:
                            dst = vd[r0:r0 + TB, (g - 8) * 256:(g - 7) * 256].rearrange("(t p) f -> p t f", p=128)
                        S.add("sp", lambda e, dst=dst, sv=sv: e.dma_start(out=dst, in_=sv),
                              r=[(svk, t) for t in range(NT)], dma="stv%d" % svk[1])
                    else:
                        for c2 in range(2):
                            col = col0 + c2 * 128
                            isgate = col >= 4608
                            stg = (sgf if isgate else sgb)[ccnt % 2]
                            stk = ("sgf" if isgate else "sgb", ccnt % 2)
                            ccnt += 1
                            for sub in range(TB // 512):
                                pp = pq[pcnt % 4]
                                ppk = ("pq", pcnt % 4)
                                pcnt += 1
                                tk = [("hT", t) for t in range(sub * 4, sub * 4 + 4)]
                                for kc in range(8):
                                    S.add("pe", lambda e, kc=kc, s=s, c2=c2, sub=sub, pp=pp: e.matmul(
                                        pp, lhsT=wi[s][:, kc, c2 * 128:(c2 + 1) * 128], rhs=hT[:, kc, sub * 512:(sub + 1) * 512],
                                        start=(kc == 0), stop=(kc == 7)), r=[("wi", s)] + tk, w=[ppk])
                                if isgate:
                                    gc = (col - 4608) // 128
                                    S.add("act", lambda e, pp=pp, stg=stg, sub=sub, gc=gc: e.activation(
                                        out=stg[:, sub * 512:(sub + 1) * 512], in_=pp, func=AF.Sigmoid, bias=bgT[:, gc:gc + 1], scale=1.0),
                                        r=[ppk, "bgT"], w=[(stk, sub)])
                                elif sub % 2 == 0:
                                    S.add("dve", lambda e, pp=pp, stg=stg, sub=sub: e.tensor_copy(out=stg[:, sub * 512:(sub + 1) * 512], in_=pp),
                                          r=[ppk], w=[(stk, sub)])
                                else:
                                    S.add("act", lambda e, pp=pp, stg=stg, sub=sub: e.activation(out=stg[:, sub * 512:(sub + 1) * 512], in_=pp, func=AF.Copy),
                                          r=[ppk], w=[(stk, sub)])
                            if col < 1024:
                                dst = qTd[col:col + 128, r0:r0 + TB]
                            elif col < 2048:
                                dst = kTd[col - 1024:col - 1024 + 128, r0:r0 + TB]
                            elif col < 4096:
                                dst = qTs[col - 3072:col - 3072 + 128, r0:r0 + TB]
                            elif col < 4352:
                                dst = kTs[col - 4096:col - 4096 + 128, r0:r0 + TB]
                            else:
                                dst = gTs[col - 4608:col - 4608 + 128, r0:r0 + TB]
                            S.add("sp", lambda e, dst=dst, stg=stg: e.dma_start(out=dst, in_=stg),
                                  r=[(stk, sub) for sub in range(TB // 512)], dma="st%s%d" % (stk[0], stk[1]))
                    if g + 3 < NG:
                        load_wi(g + 3)
                S.barrier()

    def store_oT(otok, dstT, c0, tagp, barrier=True):
        with ExitStack() as _st6:
            ost0 = _st6.enter_context(sb("ost0", [128, 8, 512], BF16))
            ost1 = _st6.enter_context(sb("ost1", [128, 8, 512], BF16))
            pot0 = _st6.enter_context(pst("pot0", [128, 1024], BF16))
            pot1 = _st6.enter_context(pst("pot1", [128, 1024], BF16))
            ost = [ost0.ap(), ost1.ap()]
            pot = [pot0.ap(), pot1.ap()]
            for qb in range(4):
                st = ost[qb % 2]
                for qi in range(4):
                    n = qb * 4 + qi
                    pp = pot[n % 2]
                    for kc in range(8):
                        S.add("pe", lambda e, n=n, kc=kc, pp=pp: e.transpose(out=pp[:, kc * 128:(kc + 1) * 128],
                                                                           in_=otok[:, n, kc * 128:(kc + 1) * 128], identity=ident),
                              r=[("otok", n), "ident"], w=[("pot", n % 2)])
                    if n % 2 == 0:
                        S.add("act", lambda e, pp=pp, st=st, qi=qi: e.activation(out=st[:, :, qi * 128:(qi + 1) * 128],
                                                                                in_=pp.rearrange("p (k t) -> p k t", k=8), func=AF.Copy),
                              r=[("pot", n % 2)], w=[("ost", qb % 2, qi)])
                    else:
                        S.add("dve", lambda e, pp=pp, st=st, qi=qi: e.tensor_copy(out=st[:, :, qi * 128:(qi + 1) * 128],
                                                                                 in_=pp.rearrange("p (k t) -> p k t", k=8)),
                              r=[("pot", n % 2)], w=[("ost", qb % 2, qi)])
                S.add("sp", lambda e, st=st, qb=qb: e.dma_start(
                    out=dstT[:, c0 + qb * 512:c0 + (qb + 1) * 512].rearrange("(k p) t -> p k t", p=128), in_=st),
                    r=[("ost", qb % 2, qi) for qi in range(4)], dma="sto%d" % (qb % 2))
            if barrier:
                S.barrier()

    def phase_B1(seq, otok):
        c0 = seq * SEQ
        NM = 3968
        if True:
            with ExitStack() as _st8:
                qT_t = _st8.enter_context(sb("qT", [128, 2, SEQ], BF16))
                kT_t = _st8.enter_context(sb("kT", [128, 2, SEQ], BF16))
                td0 = _st8.enter_context(sb("td0", [128, NM], BF16))
                td1 = _st8.enter_context(sb("td1", [128, NM], BF16))
                erl = [_st8.enter_context(sb("er%d" % i_, [128, 768], BF16)) for i_ in range(3)]
                va_t = _st8.enter_context(sb("vaug", [128, 16, 8, 129], BF16))
                dd_t = _st8.enter_context(sb("dd", [128, NM], F32))
                mhi_t = _st8.enter_context(sb("mhi", [128, NM], BF16))
                mlo_t = _st8.enter_context(sb("mlo", [128, NM], BF16))
                ih_t = _st8.enter_context(sb("ih", [128, 8, 128], BF16))
                etl = [_st8.enter_context(sb("et%d" % i_, [128, 768], BF16)) for i_ in range(6)]
                rd_t = _st8.enter_context(sb("rd", [128, 2, 4], F32))
                rl2_t = _st8.enter_context(sb("rl2", [128, 4], F32))
                t1_t = _st8.enter_context(sb("t1", [128, 3, 128], F32))
                of_t = _st8.enter_context(sb("of", [128, 3, 128], F32))
                u1_t = _st8.enter_context(sb("u1", [128, 3, 128], F32))
                jk2_t = _st8.enter_context(sb("jk2", [128, 128], BF16))
                jk2f_t = _st8.enter_context(sb("jk2f", [128, 128], F32))
                ss2_t = _st8.enter_context(sb("ss2", [128, 4], F32))
                ln2_t = _st8.enter_context(sb("ln2", [128, 4], F32))
                rs2_t = _st8.enter_context(sb("rs2", [128, 4], F32))
                pcl = [_st8.enter_context(sb("pc%d" % i_, [128, 387], F32)) for i_ in range(2)]
                psl = [_st8.enter_context(pst("ps%d" % i_, [128, 1024], F32)) for i_ in range(3)]
                pol = [_st8.enter_context(pst("po%d" % i_, [128, 512], F32)) for i_ in range(2)]
                qT, kT, vaug = qT_t.ap(), kT_t.ap(), va_t.ap()
                dd = dd_t.ap()
                Mhi, Mlo, Ih = mhi_t.ap(), mlo_t.ap(), ih_t.ap()
                td = [td0.ap(), td1.ap()]
                er = [t_.ap() for t_ in erl]
                et = [t_.ap() for t_ in etl]
                rd, rl2, t1, of = rd_t.ap(), rl2_t.ap(), t1_t.ap(), of_t.ap()
                jk2, ss2, ln2, rs2 = jk2_t.ap(), ss2_t.ap(), ln2_t.ap(), rs2_t.ap()
                u1 = u1_t.ap()
                jk2f = jk2f_t.ap()
                ps = [t_.ap() for t_ in psl]
                po = [t_.ap() for t_ in pol]
                pc = [t_.ap() for t_ in pcl]
                def ld_qk(h):
                    sl_ = h % 2
                    S.add("sp", lambda e: e.dma_start(out=qT[:, sl_, :], in_=qTd[h * 128:(h + 1) * 128, c0:c0 + SEQ]),
                          w=[("qk", sl_)], dma="bqk%d" % sl_)
                    S.add("sp", lambda e: e.dma_start(out=kT[:, sl_, :], in_=kTd[h * 128:(h + 1) * 128, c0:c0 + SEQ]),
                          w=[("qk", sl_)], dma="bqk%d" % sl_)
                ld_qk(0)
                S.add("pool", lambda e: e.memset(vaug[:, :, :, 128:129], 1.0), w=["vones"])
                for kc in range(16):
                    S.add("sp", lambda e, kc=kc: e.dma_start(
                        out=vaug[:, kc, :, 0:128],
                        in_=vd[c0 + kc * 128:c0 + (kc + 1) * 128, :].rearrange("p (h e) -> p h e", h=8)),
                        w=[("v", kc)], dma="bv")
                ld_qk(1)
                S.add("pool", lambda e: e.iota(dd, [[1, NM]], base=-1920, channel_multiplier=-1,
                                               allow_small_or_imprecise_dtypes=True), w=["dd"])
                S.add("act", lambda e: e.activation(out=dd, in_=dd, func=AF.Abs), r=["dd"], w=["dd"])
                S.add("act", lambda e: e.activation(out=Mhi, in_=dd, func=AF.Copy, scale=-1.0), r=["dd"], w=["Mhi"])
                S.add("dve", lambda e: e.scalar_tensor_tensor(out=Mlo, in0=dd, scalar=-1.0, in1=Mhi, op0=ALU.mult, op1=ALU.subtract),
                      r=["dd", "Mhi"], w=["Mlo"])
                for h in range(8):
                    S.add("dve", lambda e, h=h: e.tensor_scalar(out=Ih[:, h, :], in0=ident, scalar1=2.0 ** (2 - h), scalar2=None, op0=ALU.mult),
                          r=["ident"], w=[("Ih", h)])
                qblocks = [(0, 3), (3, 3), (6, 3), (9, 3), (12, 2), (14, 2)]
                steps = [(h, bi_, kc) for h in range(8) for bi_ in range(len(qblocks)) for kc in range(16)]
                NS = len(steps)
                LAG = 4
                NPS = 3
                NB = 3
                NBE = 6

                TDP = NM // 8

                def gen_td(h, piece):
                    sl = 2.0 ** (-(h + 1))
                    tdh = td[h % 2]
                    S.add("act", lambda e: e.activation(out=tdh[:, piece * TDP:(piece + 1) * TDP],
                                                        in_=dd[:, piece * TDP:(piece + 1) * TDP], func=AF.Exp, scale=-sl),
                          r=["dd"], w=[("td", h % 2, piece)])

                def b1_qk(i):
                    h, bi_, kc = steps[i]
                    t0_, nq = qblocks[bi_]
                    q0, k0, Wd = t0_ * 128, kc * 128, nq * 128
                    slot = i % NPS
                    pt_ = ps[slot]
                    nbuf = i % NBE
                    nbr = i % NB
                    hs = h % 2
                    m0 = q0 - k0 + 1920
                    tdh = td[h % 2]
                    use_lo = h >= 4
                    on_pe = (i % 5 == 2) if use_lo else (i % 5 in (1, 3))
                    if kc == 0 and h + 1 < 8:
                        gen_td(h + 1, bi_)
                        if bi_ == 5:
                            gen_td(h + 1, 6)
                            gen_td(h + 1, 7)
                        if bi_ == 0 and h >= 1:
                            ld_qk(h + 1)
                    S.add("pe", lambda e: e.matmul(
                        pt_[:, 0:Wd], lhsT=kT[0:64, hs, k0:k0 + 128], rhs=qT[0:64, hs, q0:q0 + Wd],
                        start=True, stop=(not on_pe)), r=[("qk", hs)], w=[("ps", slot, 0)])
                    S.add("pe", lambda e: e.matmul(
                        pt_[:, 512:512 + Wd], lhsT=kT[64:128, hs, k0:k0 + 128], rhs=qT[64:128, hs, q0:q0 + Wd],
                        start=True, stop=(not on_pe)), r=[("qk", hs)], w=[("ps", slot, 1)])
                    if on_pe:
                        for comp in range(2):
                            S.add("pe", lambda e, comp=comp: e.matmul(
                                pt_[:, comp * 512:comp * 512 + Wd], lhsT=Ih[:, h, :], rhs=Mhi[:, m0:m0 + Wd],
                                start=False, stop=(not use_lo)), r=[("Ih", h), "Mhi"], w=[("ps", slot, comp)])
                        if use_lo:
                            for comp in range(2):
                                S.add("pe", lambda e, comp=comp: e.matmul(
                                    pt_[:, comp * 512:comp * 512 + Wd], lhsT=Ih[:, h, :], rhs=Mlo[:, m0:m0 + Wd],
                                    start=False, stop=True), r=[("Ih", h), "Mlo"], w=[("ps", slot, comp)])
                        for comp in range(2):
                            S.add("act", lambda e, comp=comp: e.activation(
                                out=et[nbuf][:, comp * 384:comp * 384 + Wd],
                                in_=pt_[:, comp * 512:comp * 512 + Wd], func=AF.Exp, scale=0.125),
                                r=[("ps", slot, comp)], w=[("et", nbuf, comp)])
                    else:
                        for comp in range(2):
                            S.add("act", lambda e, comp=comp: e.activation(
                                out=er[nbr][:, comp * 384:comp * 384 + Wd],
                                in_=pt_[:, comp * 512:comp * 512 + Wd], func=AF.Exp, scale=0.125),
                                r=[("ps", slot, comp)], w=[("er", nbr, comp)])
                            S.add("dve", lambda e, comp=comp: e.tensor_tensor(
                                out=et[nbuf][:, comp * 384:comp * 384 + Wd], in0=er[nbr][:, comp * 384:comp * 384 + Wd],
                                in1=tdh[:, m0:m0 + Wd], op=ALU.mult),
                                r=[("er", nbr, comp)] + [("td", h % 2, p_) for p_ in range(8)], w=[("et", nbuf, comp)])

                def b1_av(i):
                    h, bi_, kc = steps[i]
                    t0_, nq = qblocks[bi_]
                    nbuf = i % NBE
                    for comp in range(2):
                        for qi in range(nq):
                            S.add("pe", lambda e, qi=qi, comp=comp: e.matmul(
                                po[comp][:, qi * 129:(qi + 1) * 129],
                                lhsT=et[nbuf][:, comp * 384 + qi * 128:comp * 384 + (qi + 1) * 128],
                                rhs=vaug[:, kc, h, :], start=(kc == 0 and qi == 0), stop=(kc == 15 and qi == nq - 1)),
                                r=[("et", nbuf, comp), ("v", kc), "vones"], w=[("po", comp)])
                    if kc != 15:
                        return
                    S.add("dve", lambda e: e.tensor_copy(out=pc[0][:, 0:nq * 129], in_=po[0][:, 0:nq * 129]), r=[("po", 0)], w=[("pc", 0)])
                    S.add("dve", lambda e: e.tensor_copy(out=pc[1][:, 0:nq * 129], in_=po[1][:, 0:nq * 129]), r=[("po", 1)], w=[("pc", 1)])
                    pv0 = pc[0].rearrange("p (q e) -> p q e", e=129)
                    pv1 = pc[1].rearrange("p (q e) -> p q e", e=129)
                    S.add("dve", lambda e: e.reciprocal(out=rd[:, 0, 0:nq], in_=pv0[:, 0:nq, 128]), r=[("pc", 0)], w=["rd0"])
                    S.add("dve", lambda e: e.reciprocal(out=rd[:, 1, 0:nq], in_=pv1[:, 0:nq, 128]), r=[("pc", 1)], w=["rd1"])
                    S.add("dve", lambda e: e.tensor_scalar(out=rl2[:, 0:nq], in0=rd[:, 1, 0:nq], scalar1=neglam, scalar2=None, op0=ALU.mult),
                          r=["rd1", "neglam"], w=["rl2"])
                    for qi in range(nq):
                        S.add("pool", lambda e, qi=qi: e.tensor_scalar(out=t1[:, qi, :], in0=pv0[:, qi, 0:128], scalar1=rd[:, 0, qi:qi + 1],
                                                                      scalar2=None, op0=ALU.mult),
                              r=[("pc", 0), "rd0"], w=[("t1", qi)])
                        S.add("pool", lambda e, qi=qi: e.tensor_scalar(out=u1[:, qi, :], in0=pv1[:, qi, 0:128], scalar1=rl2[:, qi:qi + 1],
                                                                       scalar2=None, op0=ALU.mult),
                              r=[("pc", 1), "rl2"], w=[("u1", qi)])
                        S.add("pool", lambda e, qi=qi: e.tensor_tensor(out=of[:, qi, :], in0=u1[:, qi, :], in1=t1[:, qi, :], op=ALU.add),
                              r=[("u1", qi), ("t1", qi)], w=[("of", qi)])

                    def part_b(nq=nq):
                        for qi in range(nq):
                            S.add("act", lambda e, qi=qi: e.activation(out=jk2, in_=of[:, qi, :], func=AF.Square, scale=128.0 ** -0.5,
                                                                       accum_out=ss2[:, qi:qi + 1]),
                                  r=[("of", qi)], w=["jk2", ("ss2", qi)])
                        S.add("act", lambda e: e.activation(out=ln2[:, 0:nq], in_=ss2[:, 0:nq], func=AF.Ln, bias=epsb, scale=1.0),
                              r=[("ss2", qi) for qi in range(nq)], w=["ln2"])
                        S.add("act", lambda e: e.activation(out=rs2[:, 0:nq], in_=ln2[:, 0:nq], func=AF.Exp, scale=-0.5),
                              r=["ln2"], w=["rs2"])

                    def part_c(nq=nq, t0_=t0_, h=h):
                        for qi in range(nq):
                            n = t0_ + qi
                            S.add("pool", lambda e, qi=qi: e.tensor_scalar(out=u1[:, qi, :], in0=of[:, qi, :], scalar1=rs2[:, qi:qi + 1],
                                                                           scalar2=None, op0=ALU.mult),
                                  r=[("of", qi), "rs2"], w=[("u1", qi)])
                            S.add("pool", lambda e, qi=qi, n=n: e.tensor_tensor(
                                out=otok[:, n, h * 128:(h + 1) * 128], in0=u1[:, qi, :], in1=gsub, op=ALU.mult),
                                r=[("u1", qi), "gsub"], w=[("otok", n, h)])
                    deferred.setdefault(i + 2, []).append(part_b)
                    deferred.setdefault(i + 5, []).append(part_c)

                deferred = {}
                for p_ in range(8):
                    gen_td(0, p_)
                for j in range(NS + LAG):
                    if j < NS:
                        b1_qk(j)
                    i = j - LAG
                    if i >= 0:
                        b1_av(i)
                        for f_ in deferred.pop(i, []):
                            f_()
                for k_ in sorted(deferred):
                    for f_ in deferred[k_]:
                        f_()
                S.barrier()
            pass

    def phase_B2(seq, pre=None):
        c0 = seq * SEQ
        with ExitStack() as _st9:
            otok_t = _st9.enter_context(sb("otok2", [128, 16, D], BF16))
            otok = otok_t.ap()
            with ExitStack() as _st10:
                q_t = _st10.enter_context(sb("qsw", [128, 8, SEQ], BF16))
                k_t = _st10.enter_context(sb("ksw", [128, 4, SEQ], BF16))
                v_t = _st10.enter_context(sb("vsw", [128, 16, 4, 65], BF16))
                er0 = _st10.enter_context(sb("er0", [128, 3, 512], BF16))
                er1 = _st10.enter_context(sb("er1", [128, 3, 512], BF16))
                et0 = _st10.enter_context(sb("et0", [128, 3, 512], BF16))
                et1 = _st10.enter_context(sb("et1", [128, 3, 512], BF16))
                dn0 = _st10.enter_context(sb("dn0", [128, 4], F32))
                dn1 = _st10.enter_context(sb("dn1", [128, 4], F32))
                rn0 = _st10.enter_context(sb("rn0", [128, 4], F32))
                rn1 = _st10.enter_context(sb("rn1", [128, 4], F32))
                pssl = [[_st10.enter_context(pst("pss%d%d" % (a_, b_), [128, 512], F32)) for b_ in range(2)] for a_ in range(2)]
                pos0 = _st10.enter_context(pst("pos0", [128, 512], F32))
                pos1 = _st10.enter_context(pst("pos1", [128, 512], F32))
                qsw, ksw, vsw = q_t.ap(), k_t.ap(), v_t.ap()
                er = [er0.ap(), er1.ap()]
                et = [et0.ap(), et1.ap()]
                dn = [dn0.ap(), dn1.ap()]
                rn = [rn0.ap(), rn1.ap()]
                pss = [[t_.ap() for t_ in row_] for row_ in pssl]
                pos = [pos0.ap(), pos1.ap()]
                for g in range(4):
                    for hf in range(2):
                        rw = (4 * g + 2 * hf) * 64
                        S.add("sp", lambda e, g=g, hf=hf, rw=rw: e.dma_start(
                            out=qsw[hf * 64:(hf + 1) * 64, 2 * g:2 * g + 2, :],
                            in_=qTs[rw:rw + 128, c0:c0 + SEQ].rearrange("(j d) t -> d j t", d=64)),
                            w=[("qsw", g, hf)], dma="b2l")
                for hf in range(2):
                    S.add("sp", lambda e, hf=hf: e.dma_start(
                        out=ksw[hf * 64:(hf + 1) * 64, :, :],
                        in_=kTs[:, c0:c0 + SEQ].rearrange("(g d) t -> d g t", d=64)), w=[("ksw", hf)], dma="b2l")
                S.add("pool", lambda e: e.memset(vsw[:, :, :, 64:65], 1.0), w=["vones2"])
                for kc in range(16):
                    S.add("sp", lambda e, kc=kc: e.dma_start(
                        out=vsw[:, kc, :, 0:64],
                        in_=vs[c0 + kc * 128:c0 + (kc + 1) * 128, :].rearrange("p (g e) -> p g e", g=4)),
                        w=[("vsw", kc // 4)], dma="b2l")
                if pre is not None:
                    pre()
                steps2 = [(n, g) for n in range(16) for g in range(4)]

                def b2_qk(i):
                    n, g = steps2[i]
                    par = i % 2
                    blocks = [j for j in (n - 1, n, n + 1) if 0 <= j < 16]
                    nb = len(blocks)
                    for hf in range(2):
                        for bk in range(2):
                            bis = [bi for bi in range(nb) if (bi // 2) == bk]
                            if not bis:
                                continue
                            for bi in bis:
                                j = blocks[bi]
                                for jj in range(2):
                                    first = (bi == bis[0] and jj == 0)
                                    last = (bi == bis[-1] and jj == 1)
                                    cc = (bi % 2) * 256 + jj * 128
                                    S.add("pe", lambda e, bk=bk, j=j, hf=hf, jj=jj, cc=cc, first=first, last=last: e.matmul(
                                        pss[hf][bk][:, cc:cc + 128],
                                        lhsT=ksw[hf * 64:(hf + 1) * 64, g, j * 128:(j + 1) * 128],
                                        rhs=qsw[hf * 64:(hf + 1) * 64, 2 * g + jj, n * 128:(n + 1) * 128],
                                        start=first, stop=last),
                                        r=[("qsw", g, hf), ("ksw", hf)], w=[("pss", hf, bk)])
                            for bi in bis:
                                cc = (bi % 2) * 256
                                S.add("act", lambda e, bi=bi, hf=hf, bk=bk, cc=cc: e.activation(
                                    out=er[par][:, bi, hf * 256:(hf + 1) * 256], in_=pss[hf][bk][:, cc:cc + 256],
                                    func=AF.Exp, scale=0.125),
                                    r=[("pss", hf, bk)], w=[("er", par, bi, hf)])
                    for bi, j in enumerate(blocks):
                        ri = 1 - (j - n)
                        S.add("dve", lambda e, bi=bi, ri=ri: e.tensor_tensor(
                            out=et[par][:, bi, :].rearrange("p (h q) -> p h q", h=4),
                            in0=er[par][:, bi, :].rearrange("p (h q) -> p h q", h=4),
                            in1=Tsw[:, 4 * g:4 * g + 4, ri * 128:(ri + 1) * 128], op=ALU.mult),
                            r=[("er", par, bi, 0), ("er", par, bi, 1)], w=[("et", par, bi)])

                def b2_av(i):
                    n, g = steps2[i]
                    par = i % 2
                    blocks = [j for j in (n - 1, n, n + 1) if 0 <= j < 16]
                    nb = len(blocks)
                    for hh in range(4):
                        for bi, j in enumerate(blocks):
                            S.add("pe", lambda e, hh=hh, bi=bi, j=j: e.matmul(
                                pos[par][:, hh * 65:(hh + 1) * 65], lhsT=et[par][:, bi, hh * 128:(hh + 1) * 128],
                                rhs=vsw[:, j, g, :], start=(bi == 0 and hh == 0), stop=(bi == nb - 1 and hh == 3)),
                                r=[("et", par, bi), ("vsw", j // 4), "vones2"], w=[("pos", par)])
                    pv = pos[par][:, 0:260].rearrange("p (h e) -> p h e", h=4)
                    S.add("dve", lambda e: e.tensor_tensor(
                        out=dn[par], in0=pv[:, :, 64], in1=es[:, 4 * g:4 * g + 4], op=ALU.add),
                        r=[("pos", par), "es"], w=[("dn", par)])
                    S.add("dve", lambda e: e.reciprocal(out=rn[par], in_=dn[par]), r=[("dn", par)], w=[("rn", par)])
                    S.add("dve", lambda e: e.tensor_tensor(
                        out=otok[:, n, 4 * g * 64:(4 * g + 4) * 64].rearrange("p (h e) -> p h e", h=4),
                        in0=pv[:, :, 0:64], in1=rn[par].unsqueeze(2).broadcast_to([128, 4, 64]), op=ALU.mult),
                        r=[("pos", par), ("rn", par)], w=[("otok", n, g)])

                b2_qk(0)
                for i in range(len(steps2)):
                    if i + 1 < len(steps2):
                        b2_qk(i + 1)
                    b2_av(i)
                S.barrier()
            store_oT(otok, oTs, c0, "s")

    def phase_C(blk):
        r0 = blk * TB
        with ExitStack() as _st11:
            xres_t = _st11.enter_context(sb("xres", [128, NT, D], F32))
            hT_t = _st11.enter_context(sb("hT", [128, 8, TB], BF16))
            hb0 = _st11.enter_context(sb("hb0", [128, D], BF16))
            hb1 = _st11.enter_context(sb("hb1", [128, D], BF16))
            junk_t = _st11.enter_context(sb("junk", [128, D], BF16))
            ssq_t = _st11.enter_context(sb("ssq", [128, NT], F32))
            lnv_t = _st11.enter_context(sb("lnv", [128, NT], F32))
            rstd_t = _st11.enter_context(sb("rstd", [128, NT], F32))
            xres, hT = xres_t.ap(), hT_t.ap()
            hb = [hb0.ap(), hb1.ap()]
            junk, ssq, lnv, rstd = junk_t.ap(), ssq_t.ap(), lnv_t.ap(), rstd_t.ap()
            with ExitStack() as _st12:
                oTa_t = _st12.enter_context(sb("oTa", [128, 8, TB], BF16))
                oTb_t = _st12.enter_context(sb("oTb", [128, 8, TB], BF16))
                PA_t = _st12.enter_context(sb("PA", [128, 8, D], BF16))
                PB_t = _st12.enter_context(sb("PB", [128, 8, D], BF16))
                WO_t = _st12.enter_context(sb("WO", [128, 8, D], BF16))
                mT_t = _st12.enter_context(sb("mT", [128, 8, TB], BF16))
                ga0 = _st12.enter_context(sb("ga0", [128, TB], F32))
                ga1 = _st12.enter_context(sb("ga1", [128, TB], F32))
                gb0 = _st12.enter_context(sb("gb0", [128, TB], F32))
                gb1 = _st12.enter_context(sb("gb1", [128, TB], F32))
                m10 = _st12.enter_context(sb("m10", [128, 512], F32))
                m11 = _st12.enter_context(sb("m11", [128, 512], F32))
                m20 = _st12.enter_context(sb("m20", [128, 512], F32))
                m21 = _st12.enter_context(sb("m21", [128, 512], F32))
                p0 = _st12.enter_context(pst("ptr0", [128, 1024], BF16))
                p1 = _st12.enter_context(pst("ptr1", [128, 1024], BF16))
                pa0 = _st12.enter_context(pst("pa0", [128, 512], F32))
                pa1 = _st12.enter_context(pst("pa1", [128, 512], F32))
                pb0 = _st12.enter_context(pst("pb0", [128, 512], F32))
                pb1 = _st12.enter_context(pst("pb1", [128, 512], F32))
                py0 = _st12.enter_context(pst("py0", [128, 512], F32))
                py1 = _st12.enter_context(pst("py1", [128, 512], F32))
                oTa, oTb, PA, PB, WO, mT = oTa_t.ap(), oTb_t.ap(), PA_t.ap(), PB_t.ap(), WO_t.ap(), mT_t.ap()
                ga = [ga0.ap(), ga1.ap()]
                gbt = [gb0.ap(), gb1.ap()]
                m1 = [m10.ap(), m11.ap()]
                m2 = [m20.ap(), m21.ap()]
                ptr = [p0.ap(), p1.ap()]
                pa = [pa0.ap(), pa1.ap()]
                pb = [pb0.ap(), pb1.ap()]
                py = [py0.ap(), py1.ap()]
                S.add("sp", lambda e: e.dma_start(out=oTa, in_=oTd[:, r0:r0 + TB].rearrange("(k p) t -> p k t", p=128)), w=["oTa"], dma="co")
                S.add("sp", lambda e: e.dma_start(out=oTb, in_=oTs[:, r0:r0 + TB].rearrange("(k p) t -> p k t", p=128)), w=["oTb"], dma="co")
                for hlf in range(2):
                    S.add("pool", lambda e, hlf=hlf: e.dma_start(out=PA[:, hlf * 4:(hlf + 1) * 4, :],
                                                                in_=W["w_proj_da"][hlf * 512:(hlf + 1) * 512, :].rearrange("(k p) f -> p k f", p=128)),
                          w=[("PA", hlf)], dma="cw0")
                    S.add("pool", lambda e, hlf=hlf: e.dma_start(out=PB[:, hlf * 4:(hlf + 1) * 4, :],
                                                                in_=W["w_proj_swa"][hlf * 512:(hlf + 1) * 512, :].rearrange("(k p) f -> p k f", p=128)),
                          w=[("PB", hlf)], dma="cw0")
                for hlf in range(2):
                    S.add("pool", lambda e, hlf=hlf: e.dma_start(out=WO[:, hlf * 4:(hlf + 1) * 4, :],
                                                                in_=W["w_out"][hlf * 512:(hlf + 1) * 512, :].rearrange("(k p) f -> p k f", p=128)),
                          w=[("WO", hlf)], dma="cw1")

                def load_g(c):
                    s = c % 2
                    S.add("sp", lambda e: e.dma_start(out=ga[s], in_=gTs[c * 128:(c + 1) * 128, r0:r0 + TB]), w=[("ga", s)], dma="cga%d" % s)
                    S.add("sp", lambda e: e.dma_start(out=gbt[s], in_=gTs[1024 + c * 128:1024 + (c + 1) * 128, r0:r0 + TB]), w=[("gbt", s)], dma="cgb%d" % s)
                load_g(0)
                load_g(1)
                for t in range(NT):
                    S.add("sp", lambda e, t=t: e.dma_start(out=xres[:, t, :], in_=x1s[r0 + t * 128:r0 + (t + 1) * 128, :]),
                          w=[("x", t)], dma="x%d" % t)
                cnt = 0
                for c in range(8):
                    s = c % 2
                    for sub in range(TB // 512):
                        par = cnt % 2
                        cnt += 1
                        for k in range(8):
                            S.add("pe", lambda e, k=k, c=c, sub=sub, par=par: e.matmul(
                                pa[par], lhsT=PA[:, k, c * 128:(c + 1) * 128], rhs=oTa[:, k, sub * 512:(sub + 1) * 512],
                                start=(k == 0), stop=(k == 7)), r=[("PA", k // 4), "oTa"], w=[("pa", par)])
                        for k in range(8):
                            S.add("pe", lambda e, k=k, c=c, sub=sub, par=par: e.matmul(
                                pb[par], lhsT=PB[:, k, c * 128:(c + 1) * 128], rhs=oTb[:, k, sub * 512:(sub + 1) * 512],
                                start=(k == 0), stop=(k == 7)), r=[("PB", k // 4), "oTb"], w=[("pb", par)])
                        S.add("dve", lambda e, par=par, s=s, sub=sub: e.tensor_tensor(
                            out=m1[par], in0=pa[par], in1=ga[s][:, sub * 512:(sub + 1) * 512], op=ALU.mult),
                            r=[("pa", par), ("ga", s)], w=[("m1", par)])
                        S.add("dve", lambda e, par=par, s=s, sub=sub: e.tensor_tensor(
                            out=m2[par], in0=pb[par], in1=gbt[s][:, sub * 512:(sub + 1) * 512], op=ALU.mult),
                            r=[("pb", par), ("gbt", s)], w=[("m2", par)])
                        S.add("pool", lambda e, par=par, c=c, sub=sub: e.tensor_tensor(
                            out=mT[:, c, sub * 512:(sub + 1) * 512], in0=m1[par], in1=m2[par], op=ALU.add),
                            r=[("m1", par), ("m2", par)], w=[("mT", c, sub)])
                    if c + 2 < 8:
                        load_g(c + 2)
                cnt = 0
                for t in range(NT):
                    for half in range(2):
                        par = cnt % 2
                        cnt += 1
                        for c in range(8):
                            S.add("pe", lambda e, t=t, half=half, c=c, par=par: e.matmul(
                                py[par], lhsT=mT[:, c, t * 128:(t + 1) * 128], rhs=WO[:, c, half * 512:(half + 1) * 512],
                                start=(c == 0), stop=(c == 7)), r=[("mT", c, t // 4), ("WO", c // 4)], w=[("py", par)])
                        S.add("dve", lambda e, t=t, half=half, par=par: e.tensor_tensor(
                            out=xres[:, t, half * 512:(half + 1) * 512], in0=py[par], in1=xres[:, t, half * 512:(half + 1) * 512], op=ALU.add),
                            r=[("py", par), ("x", t)], w=[("x", t)])
                norm_to_hT(xres, 2, hT, hb, ptr, junk, ssq, lnv, rstd)
                S.barrier()
            with ExitStack() as _st13:
                pg0 = _st13.enter_context(pst("pg0", [128, 512], F32))
                pg1 = _st13.enter_context(pst("pg1", [128, 512], F32))
                pu0 = _st13.enter_context(pst("pu0", [128, 512], F32))
                pu1 = _st13.enter_context(pst("pu1", [128, 512], F32))
                py0 = _st13.enter_context(pst("py0", [128, 512], F32))
                py1 = _st13.enter_context(pst("py1", [128, 512], F32))
                ofl = [_st13.enter_context(sb("of%d" % i_, [128, D], F32)) for i_ in range(3)]
                pg = [pg0.ap(), pg1.ap()]
                pu = [pu0.ap(), pu1.ap()]
                py = [py0.ap(), py1.ap()]
                ofb = [t_.ap() for t_ in ofl]
                ffn(xres, hT, W["ffn2_gate"], W["ffn2_up"], W["ffn2_down"], pg, pu, py)
                def fstats(t):
                    S.add("act", lambda e: e.activation(out=junk, in_=xres[:, t, :], func=AF.Square, scale=1.0 / 32.0,
                                                        accum_out=ssq[:, t:t + 1]), r=[("x", t)], w=["junk", ("ssq", t)])
                    S.add("act", lambda e: e.activation(out=lnv[:, t:t + 1], in_=ssq[:, t:t + 1], func=AF.Ln, bias=epsb, scale=1.0),
                          r=[("ssq", t)], w=[("lnv", t)])
                    S.add("act", lambda e: e.activation(out=rstd[:, t:t + 1], in_=lnv[:, t:t + 1], func=AF.Exp, scale=-0.5),
                          r=[("lnv", t)], w=[("rstd", t)])
                for t in range(NT):
                    fstats(t)
                    S.add("dve", lambda e, t=t: e.scalar_tensor_tensor(out=ofb[t % 3], in0=xres[:, t, :], scalar=rstd[:, t:t + 1],
                                                                     in1=gb[:, 3, :], op0=ALU.mult, op1=ALU.mult),
                          r=[("x", t), ("rstd", t), ("gb", 3)], w=[("ofb", t % 3)])
                    S.add("sp", lambda e, t=t: e.dma_start(out=out[r0 + t * 128:r0 + (t + 1) * 128, :], in_=ofb[t % 3]),
                          r=[("ofb", t % 3)], dma="sto%d" % (t % 3))
                S.barrier()

    setup()
    for seq in range(nseq):
        if "A" in phases:
            phase_A(2 * seq)
            phase_A(2 * seq + 1)
        if "B" in phases:
            with ExitStack() as _stb:
                otok1 = _stb.enter_context(sb("otok", [128, 16, D], BF16)).ap()
                phase_B1(seq, otok1)
                phase_B2(seq, pre=lambda: store_oT(otok1, oTd, seq * SEQ, "d", barrier=False))
        else:
            if "B1" in phases:
                with ExitStack() as _stb:
                    otok1 = _stb.enter_context(sb("otok", [128, 16, D], BF16)).ap()
                    phase_B1(seq, otok1)
                    store_oT(otok1, oTd, seq * SEQ, "d")
            if "B2" in phases:
                phase_B2(seq)
        if "C" in phases:
            phase_C(2 * seq)
            phase_C(2 * seq + 1)
    S.finish()
    return nc


_WNAMES = ["norm_ffn1", "ffn1_gate", "ffn1_up", "ffn1_down", "norm_mix", "w_in", "b_gate", "da_lambda",
           "da_subnorm", "swa_sink", "w_proj_da", "w_proj_swa", "w_out", "norm_ffn2", "ffn2_gate", "ffn2_up",
           "ffn2_down", "norm_final"]


def prep_weights(inputs):
    w = {}
    for nm in _WNAMES:
        a = np.asarray(inputs[nm], dtype=np.float32)
        if nm != "norm_final":
            a = a[0]
        if nm in ("b_gate", "da_lambda"):
            a = a.reshape(-1)
        w[nm] = np.ascontiguousarray(a)
    return w


def kernel(**inputs):
    x = np.asarray(inputs["x"], dtype=np.float32)
    w = prep_weights(inputs)
    nc = build_program()
    in_maps = []
    for c in range(NCORES):
        m = dict(w)
        m["x"] = np.ascontiguousarray(x[2 * c:2 * c + 2].reshape(T, D))
        in_maps.append(m)
    res = run_bass_kernel_spmd(nc, in_maps, core_ids=list(range(NCORES)))
    outs = [np.asarray(r["out"], dtype=np.float32).reshape(2, SEQ, D) for r in res.results]
    return np.concatenate(outs, axis=0)
```

```python
import numpy as np
from contextlib import ExitStack
import concourse.bass as bass
import concourse.mybir as mybir
from concourse.bass_utils import run_bass_kernel_spmd

F32 = mybir.dt.float32
BF16 = mybir.dt.bfloat16
AF = mybir.ActivationFunctionType
ALU = mybir.AluOpType
AX = mybir.AxisListType

NCORES = 8
T = 4096
SEQ = 2048
D = 1024
DFF = 2816
NFF = 22
INC = 6656
TB = 1024
NT = TB // 128


class Sched:
    ENG = ("pe", "act", "dve", "pool", "sp")

    def __init__(self, nc):
        self.nc = nc
        self.eng = {"pe": nc.tensor, "act": nc.scalar, "dve": nc.vector,
                    "pool": nc.gpsimd, "sp": nc.sync}
        self.ops = []
        self.last_w = {}
        self.readers = {}
        self.sems = {}
        self.counts = {}
        self.sig = {}
        self.waited = {e: {} for e in self.ENG}
        self.emitted = 0
        self.last_on = {}

    def add(self, eng, fn, r=(), w=(), dma=None, barrier=False):
        idx = len(self.ops)
        deps = {}
        for k in r:
            lw = self.last_w.get(k)
            if lw is not None:
                deps[lw] = True
        for k in w:
            lw = self.last_w.get(k)
            if lw is not None:
                if k in ("junk", "jk2"):
                    deps[lw] = True
                deps.setdefault(lw, False)
            for rd in self.readers.get(k, ()):
                deps.setdefault(rd, False)
        for k in r:
            self.readers.setdefault(k, []).append(idx)
        for k in w:
            self.last_w[k] = idx
            self.readers[k] = []
        deps.pop(idx, None)
        self.ops.append([eng, fn, deps, dma, barrier])
        if dma is None:
            self.last_on[eng] = idx
        return idx

    def barrier(self):
        lasts = dict(self.last_on)
        for e in self.ENG:
            idx = self.add(e, lambda en: en.nop(), barrier=True)
            for e2, li in lasts.items():
                if li >= self.emitted:
                    self.ops[idx][2][li] = True
        self.last_w = {}
        self.readers = {}
        self.emit()

    def _getsem(self, key):
        if key not in self.sems:
            self.sems[key] = self.nc.alloc_semaphore(name="s%d" % len(self.sems))
            self.counts[key] = 0
        return self.sems[key]

    def emit(self):
        ops = self.ops
        n = len(ops)
        start = self.emitted
        need = {}
        for i in range(start, n):
            eng, fn, deps, dma, bar = ops[i]
            for d, israw in deps.items():
                deng, _, _, ddma, _ = ops[d]
                if ddma is not None:
                    continue
                if deng != eng or (israw and eng != "pe"):
                    need[d] = True
        for i in range(start, n):
            eng, fn, deps, dma, bar = ops[i]
            e = self.eng[eng]
            wl = {}
            for d, israw in deps.items():
                deng, _, _, ddma, _ = ops[d]
                if ddma is not None:
                    key = ("d", ddma)
                    val = self.counts[key]
                elif deng != eng or (israw and eng != "pe"):
                    key, val = self.sig[d]
                else:
                    continue
                if wl.get(key, 0) < val:
                    wl[key] = val
            if bar:
                for key, val in self.counts.items():
                    if key[0] == "d" and val > 0:
                        wl[key] = val
            for key, val in wl.items():
                if self.waited[eng].get(key, 0) >= val:
                    continue
                self.waited[eng][key] = val
                e.wait_ge(self.sems[key], val)
            ins = fn(e)
            if dma is not None:
                key = ("d", dma)
                s = self._getsem(key)
                self.counts[key] += 16
                ins.then_inc(s, 16)
                self.sig[i] = (key, self.counts[key])
            elif need.get(i):
                key = ("e", eng)
                s = self._getsem(key)
                self.counts[key] += 1
                ins.then_inc(s, 1)
                self.sig[i] = (key, self.counts[key])
        self.emitted = n

    def finish(self):
        self.barrier()


def build_program(debug=False, phases=("A", "B", "C"), nseq=2):
    nc = bass.Bass("TRN2", target_bir_lowering=False)

    def din(name, shape, dt=F32):
        return nc.dram_tensor(name, shape, dt, kind="ExternalInput").ap()

    skind = "ExternalOutput" if debug else "Internal"

    def dscr(name, shape, dt):
        return nc.dram_tensor(name, shape, dt, kind=skind).ap()

    x_in = din("x", [T, D])
    w_names = {}
    for nm, shp in [("norm_ffn1", [D]), ("ffn1_gate", [D, DFF]), ("ffn1_up", [D, DFF]), ("ffn1_down", [DFF, D]),
                    ("norm_mix", [D]), ("w_in", [D, INC]), ("b_gate", [2 * D]), ("da_lambda", [256]),
                    ("da_subnorm", [128]), ("swa_sink", [16]), ("w_proj_da", [D, D]), ("w_proj_swa", [D, D]),
                    ("w_out", [D, D]), ("norm_ffn2", [D]), ("ffn2_gate", [D, DFF]), ("ffn2_up", [D, DFF]),
                    ("ffn2_down", [DFF, D]), ("norm_final", [D])]:
        w_names[nm] = din(nm, shp)
    W = w_names
    out = nc.dram_tensor("out", [T, D], F32, kind="ExternalOutput").ap()

    x1s = dscr("x1s", [T, D], F32)
    qTd = dscr("qTd", [D, T], BF16)
    kTd = dscr("kTd", [D, T], BF16)
    vd = dscr("vd", [T, D], BF16)
    qTs = dscr("qTs", [D, T], BF16)
    kTs = dscr("kTs", [256, T], BF16)
    vs = dscr("vs", [T, 256], BF16)
    gTs = dscr("gTs", [2 * D, T], F32)
    oTd = dscr("oTd", [D, T], BF16)
    oTs = dscr("oTs", [D, T], BF16)

    S = Sched(nc)

    uid = [0]

    def sb(name, shape, dt):
        uid[0] += 1
        return nc.sbuf_tensor("%s_%d" % (name, uid[0]), shape, dt)

    def pst(name, shape, dt):
        uid[0] += 1
        return nc.psum_tensor("%s_%d" % (name, uid[0]), shape, dt)

    ident = nc.alloc_sbuf_tensor("ident", [128, 128], BF16).ap()
    onesb = nc.alloc_sbuf_tensor("onesb", [128, 128], BF16).ap()
    epsb = nc.alloc_sbuf_tensor("epsb", [128, 1], F32).ap()
    gb = nc.alloc_sbuf_tensor("gb", [128, 4, D], F32).ap()
    bgT = nc.alloc_sbuf_tensor("bgT", [128, 16], F32).ap()
    lt = nc.alloc_sbuf_tensor("lt", [128, 256], F32).ap()
    ltmp = nc.alloc_sbuf_tensor("ltmp", [128, 64], F32).ap()
    s12 = nc.alloc_sbuf_tensor("s12", [128, 2], F32).ap()
    e12 = nc.alloc_sbuf_tensor("e12", [128, 2], F32).ap()
    nl0 = nc.alloc_sbuf_tensor("nl0", [128, 1], F32).ap()
    neglam = nc.alloc_sbuf_tensor("neglam", [128, 1], F32).ap()
    gsub0 = nc.alloc_sbuf_tensor("gsub0", [128, 128], F32).ap()
    gsub = nc.alloc_sbuf_tensor("gsub", [128, 128], F32).ap()
    sk0 = nc.alloc_sbuf_tensor("sk0", [128, 16], F32).ap()
    es = nc.alloc_sbuf_tensor("es", [128, 16], F32).ap()
    Tsw = nc.alloc_sbuf_tensor("Tsw", [128, 16, 384], BF16).ap()

    def setup():
        S.add("pool", lambda e: e.memset(onesb, 1.0), w=["onesb"])
        S.add("pool", lambda e: e.affine_select(out=ident, in_=onesb, pattern=[[1, 128]], compare_op=ALU.is_equal,
                                               fill=0.0, base=0, channel_multiplier=-1), r=["onesb"], w=["ident"])
        S.add("pool", lambda e: e.memset(epsb, 1e-6), w=["epsb"])
        for i, nm in enumerate(["norm_ffn1", "norm_mix", "norm_ffn2", "norm_final"]):
            S.add("sp", lambda e, i=i, nm=nm: e.dma_start(out=gb[:, i, :], in_=W[nm].partition_broadcast(128)),
                  w=[("gb", i)], dma="setup")
        S.add("sp", lambda e: e.dma_start(out=bgT, in_=W["b_gate"].rearrange("(c p) -> p c", p=128),
                                         allow_slow_non_contiguous=True), w=["bgT"], dma="setup")
        S.add("sp", lambda e: e.dma_start(out=lt, in_=W["da_lambda"].partition_broadcast(128)), w=["lt"], dma="setup")
        S.add("sp", lambda e: e.dma_start(out=gsub0, in_=W["da_subnorm"].partition_broadcast(128)), w=["gsub0"], dma="setup")
        S.add("sp", lambda e: e.dma_start(out=sk0, in_=W["swa_sink"].partition_broadcast(128)), w=["sk0"], dma="setup")
        S.add("dve", lambda e: e.tensor_tensor(out=ltmp, in0=lt[:, 0:64], in1=lt[:, 64:128], op=ALU.mult), r=["lt"], w=["ltmp"])
        S.add("dve", lambda e: e.reduce_sum(out=s12[:, 0:1], in_=ltmp, axis=AX.X), r=["ltmp"], w=["s12a"])
        S.add("dve", lambda e: e.tensor_tensor(out=ltmp, in0=lt[:, 128:192], in1=lt[:, 192:256], op=ALU.mult), r=["lt", "s12a"], w=["ltmp"])
        S.add("dve", lambda e: e.reduce_sum(out=s12[:, 1:2], in_=ltmp, axis=AX.X), r=["ltmp"], w=["s12b"])
        S.add("act", lambda e: e.activation(out=e12, in_=s12, func=AF.Exp), r=["s12a", "s12b"], w=["e12"])
        S.add("dve", lambda e: e.tensor_tensor(out=nl0, in0=e12[:, 1:2], in1=e12[:, 0:1], op=ALU.subtract), r=["e12"], w=["nl0"])
        S.add("dve", lambda e: e.tensor_scalar(out=neglam, in0=nl0, scalar1=-0.2, scalar2=None, op0=ALU.add), r=["nl0"], w=["neglam"])
        S.add("dve", lambda e: e.tensor_scalar(out=gsub, in0=gsub0, scalar1=0.8, scalar2=None, op0=ALU.mult), r=["gsub0"], w=["gsub"])
        S.add("act", lambda e: e.activation(out=es, in_=sk0, func=AF.Exp), r=["sk0"], w=["es"])
        with ExitStack() as _st1:
            dswi_t = _st1.enter_context(sb("dswi", [128, 384], F32))
            dswa_t = _st1.enter_context(sb("dswa", [128, 384], F32))
            dswb_t = _st1.enter_context(sb("dswb", [128, 384], F32))
            dswi, dswa, dswb = dswi_t.ap(), dswa_t.ap(), dswb_t.ap()
            S.add("pool", lambda e: e.iota(dswi, [[1, 384]], base=-128, channel_multiplier=-1,
                                           allow_small_or_imprecise_dtypes=True), w=["dswi"])
            S.add("act", lambda e: e.activation(out=dswa, in_=dswi, func=AF.Abs), r=["dswi"], w=["dswa"])
            S.add("pool", lambda e: e.affine_select(out=dswb, in_=dswa, pattern=[[1, 384]], compare_op=ALU.is_ge,
                                                   fill=1.0e6, base=0, channel_multiplier=-1), r=["dswa"], w=["dswb"])
            S.add("pool", lambda e: e.affine_select(out=dswi, in_=dswb, pattern=[[-1, 384]], compare_op=ALU.is_ge,
                                                   fill=1.0e6, base=256, channel_multiplier=1), r=["dswb"], w=["dswi2"])
            for h in range(16):
                sl = 2.0 ** (-8.0 * (h + 1) / 16.0)
                S.add("act", lambda e, h=h, sl=sl: e.activation(out=Tsw[:, h, :], in_=dswi, func=AF.Exp, scale=-sl),
                      r=["dswi2"], w=[("Tsw", h)])
            S.barrier()

    def norm_to_hT(xres, gi, hT, hb, ptr, junk, ssq, lnv, rstd, store=None):
        def stats(t):
            S.add("act", lambda e: e.activation(out=junk, in_=xres[:, t, :], func=AF.Square, scale=1.0 / 32.0,
                                                accum_out=ssq[:, t:t + 1]),
                  r=[("x", t)], w=["junk", ("ssq", t)])
            S.add("act", lambda e: e.activation(out=lnv[:, t:t + 1], in_=ssq[:, t:t + 1], func=AF.Ln, bias=epsb, scale=1.0),
                  r=[("ssq", t)], w=[("lnv", t)])
            S.add("act", lambda e: e.activation(out=rstd[:, t:t + 1], in_=lnv[:, t:t + 1], func=AF.Exp, scale=-0.5),
                  r=[("lnv", t)], w=[("rstd", t)])
        stats(0)
        stats(1)
        for t in range(NT):
            if t + 2 < NT:
                stats(t + 2)
            hbt = hb[t % 2]
            pt = ptr[t % 2]
            S.add("dve", lambda e, t=t, hbt=hbt: e.scalar_tensor_tensor(out=hbt, in0=xres[:, t, :], scalar=rstd[:, t:t + 1],
                                                                     in1=gb[:, gi, :], op0=ALU.mult, op1=ALU.mult),
                  r=[("x", t), ("rstd", t), ("gb", gi)], w=[("hb", t % 2)])
            for kc in range(8):
                S.add("pe", lambda e, kc=kc, hbt=hbt, pt=pt: e.transpose(out=pt[:, kc * 128:(kc + 1) * 128],
                                                                          in_=hbt[:, kc * 128:(kc + 1) * 128], identity=ident),
                      r=[("hb", t % 2), "ident"], w=[("ptr", t % 2)])
            S.add("act", lambda e, t=t, pt=pt: e.activation(out=hT[:, :, t * 128:(t + 1) * 128],
                                                           in_=pt.rearrange("p (k t) -> p k t", k=8), func=AF.Copy),
                  r=[("ptr", t % 2)], w=[("hT", t)])
            if store is not None:
                store(t)

    def ffn(xres, hT, wg_d, wu_d, wd_d, pg, pu, py):
        with ExitStack() as _st2:
            aT_t = _st2.enter_context(sb("aT", [128, NFF, TB], BF16))
            wd_t = _st2.enter_context(sb("wdb", [128, NFF, D], BF16))
            wg0 = _st2.enter_context(sb("wg0", [128, 8, 256], BF16))
            wg1 = _st2.enter_context(sb("wg1", [128, 8, 256], BF16))
            wu0 = _st2.enter_context(sb("wu0", [128, 8, 256], BF16))
            wu1 = _st2.enter_context(sb("wu1", [128, 8, 256], BF16))
            sg0 = _st2.enter_context(sb("sg0", [128, 512], F32))
            sg1 = _st2.enter_context(sb("sg1", [128, 512], F32))
            aT, wdb = aT_t.ap(), wd_t.ap()
            wg = [wg0.ap(), wg1.ap()]
            wu = [wu0.ap(), wu1.ap()]
            sg = [sg0.ap(), sg1.ap()]
            NG = 11

            def load_gu(g):
                s = g % 2
                S.add("pool", lambda e: e.dma_start(out=wg[s], in_=wg_d[:, g * 256:(g + 1) * 256].rearrange("(kc p) f -> p kc f", p=128)),
                      w=[("wg", s)], dma="wg%d" % s)
                S.add("pool", lambda e: e.dma_start(out=wu[s], in_=wu_d[:, g * 256:(g + 1) * 256].rearrange("(kc p) f -> p kc f", p=128)),
                      w=[("wu", s)], dma="wu%d" % s)

            def load_wd(i):
                S.add("pool", lambda e: e.dma_start(out=wdb[:, 2 * i:2 * i + 2, :],
                                                   in_=wd_d[i * 256:(i + 1) * 256, :].rearrange("(c p) f -> p c f", p=128)),
                      w=[("wd", i)], dma="wd")

            load_gu(0)
            load_gu(1)
            cnt = 0
            for g in range(NG):
                s = g % 2
                for c2 in range(2):
                    ffc = g * 2 + c2
                    for sub in range(TB // 512):
                        par = cnt % 2
                        cnt += 1
                        tk = [("hT", t) for t in range(sub * 4, sub * 4 + 4)]
                        for kc in range(8):
                            S.add("pe", lambda e, kc=kc, s=s, c2=c2, sub=sub, par=par: e.matmul(
                                pg[par], lhsT=wg[s][:, kc, c2 * 128:(c2 + 1) * 128], rhs=hT[:, kc, sub * 512:(sub + 1) * 512],
                                start=(kc == 0), stop=(kc == 7)), r=[("wg", s)] + tk, w=[("pg", par)])
                        for kc in range(8):
                            S.add("pe", lambda e, kc=kc, s=s, c2=c2, sub=sub, par=par: e.matmul(
                                pu[par], lhsT=wu[s][:, kc, c2 * 128:(c2 + 1) * 128], rhs=hT[:, kc, sub * 512:(sub + 1) * 512],
                                start=(kc == 0), stop=(kc == 7)), r=[("wu", s)] + tk, w=[("pu", par)])
                        S.add("act", lambda e, par=par: e.activation(out=sg[par], in_=pg[par], func=AF.Silu),
                              r=[("pg", par)], w=[("sg", par)])
                        S.add("dve", lambda e, par=par, ffc=ffc, sub=sub: e.tensor_tensor(
                            out=aT[:, ffc, sub * 512:(sub + 1) * 512], in0=sg[par], in1=pu[par], op=ALU.mult),
                            r=[("sg", par), ("pu", par)], w=[("aT", ffc, sub)])
                if g + 2 < NG:
                    load_gu(g + 2)
                load_wd(g)
            cnt = 0
            for t in range(NT):
                for half in range(2):
                    par = cnt % 2
                    cnt += 1
                    for ffc in range(NFF):
                        S.add("pe", lambda e, t=t, half=half, ffc=ffc, par=par: e.matmul(
                            py[par], lhsT=aT[:, ffc, t * 128:(t + 1) * 128], rhs=wdb[:, ffc, half * 512:(half + 1) * 512],
                            start=(ffc == 0), stop=(ffc == NFF - 1)),
                            r=[("aT", ffc, t // 4), ("wd", ffc // 2)], w=[("py", par)])
                    S.add("dve", lambda e, t=t, half=half, par=par: e.scalar_tensor_tensor(
                        out=xres[:, t, half * 512:(half + 1) * 512], in0=py[par], scalar=0.5,
                        in1=xres[:, t, half * 512:(half + 1) * 512], op0=ALU.mult, op1=ALU.add),
                        r=[("py", par), ("x", t)], w=[("x", t)])

    def phase_A(blk):
        r0 = blk * TB
        with ExitStack() as _st3:
            xres_t = _st3.enter_context(sb("xres", [128, NT, D], F32))
            hT_t = _st3.enter_context(sb("hT", [128, 8, TB], BF16))
            hb0 = _st3.enter_context(sb("hb0", [128, D], BF16))
            hb1 = _st3.enter_context(sb("hb1", [128, D], BF16))
            junk_t = _st3.enter_context(sb("junk", [128, D], BF16))
            ssq_t = _st3.enter_context(sb("ssq", [128, NT], F32))
            lnv_t = _st3.enter_context(sb("lnv", [128, NT], F32))
            rstd_t = _st3.enter_context(sb("rstd", [128, NT], F32))
            xres, hT = xres_t.ap(), hT_t.ap()
            hb = [hb0.ap(), hb1.ap()]
            junk, ssq, lnv, rstd = junk_t.ap(), ssq_t.ap(), lnv_t.ap(), rstd_t.ap()
            with ExitStack() as _st4:
                p0 = _st4.enter_context(pst("ptr0", [128, 1024], BF16))
                p1 = _st4.enter_context(pst("ptr1", [128, 1024], BF16))
                pg0 = _st4.enter_context(pst("pg0", [128, 512], F32))
                pg1 = _st4.enter_context(pst("pg1", [128, 512], F32))
                pu0 = _st4.enter_context(pst("pu0", [128, 512], F32))
                pu1 = _st4.enter_context(pst("pu1", [128, 512], F32))
                py0 = _st4.enter_context(pst("py0", [128, 512], F32))
                py1 = _st4.enter_context(pst("py1", [128, 512], F32))
                ptr = [p0.ap(), p1.ap()]
                pg = [pg0.ap(), pg1.ap()]
                pu = [pu0.ap(), pu1.ap()]
                py = [py0.ap(), py1.ap()]
                for t in range(NT):
                    S.add("sp", lambda e, t=t: e.dma_start(out=xres[:, t, :], in_=x_in[r0 + t * 128:r0 + (t + 1) * 128, :]),
                          w=[("x", t)], dma="x%d" % t)
                norm_to_hT(xres, 0, hT, hb, ptr, junk, ssq, lnv, rstd)
                ffn(xres, hT, W["ffn1_gate"], W["ffn1_up"], W["ffn1_down"], pg, pu, py)

                def store_x1(t):
                    S.add("sp", lambda e, t=t: e.dma_start(out=x1s[r0 + t * 128:r0 + (t + 1) * 128, :], in_=xres[:, t, :]),
                          r=[("x", t)], dma="st")
                norm_to_hT(xres, 1, hT, hb, ptr, junk, ssq, lnv, rstd, store=store_x1)
                S.barrier()
            with ExitStack() as _st5:
                wi0 = _st5.enter_context(sb("wi0", [128, 8, 256], BF16))
                wi1 = _st5.enter_context(sb("wi1", [128, 8, 256], BF16))
                wi2 = _st5.enter_context(sb("wi2", [128, 8, 256], BF16))
                sgb0 = _st5.enter_context(sb("sgb0", [128, TB], BF16))
                sgb1 = _st5.enter_context(sb("sgb1", [128, TB], BF16))
                sgf0 = _st5.enter_context(sb("sgf0", [128, TB], F32))
                sgf1 = _st5.enter_context(sb("sgf1", [128, TB], F32))
                svb0 = _st5.enter_context(sb("svb0", [128, NT, 256], BF16))
                svb1 = _st5.enter_context(sb("svb1", [128, NT, 256], BF16))
                pq0 = _st5.enter_context(pst("pq0", [128, 512], F32))
                pq1 = _st5.enter_context(pst("pq1", [128, 512], F32))
                pq2 = _st5.enter_context(pst("pq2", [128, 512], F32))
                pq3 = _st5.enter_context(pst("pq3", [128, 512], F32))
                wi = [wi0.ap(), wi1.ap(), wi2.ap()]
                sgb = [sgb0.ap(), sgb1.ap()]
                sgf = [sgf0.ap(), sgf1.ap()]
                svb = [svb0.ap(), svb1.ap()]
                pq = [pq0.ap(), pq1.ap(), pq2.ap(), pq3.ap()]
                NG = 26
                allh = [("hT", t) for t in range(NT)]

                def load_wi(g):
                    s = g % 3
                    S.add("pool", lambda e: e.dma_start(out=wi[s], in_=W["w_in"][:, g * 256:(g + 1) * 256].rearrange("(kc p) f -> p kc f", p=128)),
                          w=[("wi", s)], dma="wi%d" % s)
                load_wi(0)
                load_wi(1)
                load_wi(2)
                pcnt = 0
                ccnt = 0
                vcnt = 0
                for g in range(NG):
                    s = g % 3
                    col0 = g * 256
                    if (8 <= g < 12) or g == 17:
                        sv = svb[vcnt % 2]
                        svk = ("svb", vcnt % 2)
                        vcnt += 1
                        for t in range(NT):
                            pp = pq[pcnt % 4]
                            ppk = ("pq", pcnt % 4)
                            pcnt += 1
                            for kc in range(8):
                                S.add("pe", lambda e, kc=kc, t=t, s=s, pp=pp: e.matmul(
                                    pp[:, 0:256], lhsT=hT[:, kc, t * 128:(t + 1) * 128], rhs=wi[s][:, kc, :],
                                    start=(kc == 0), stop=(kc == 7)), r=[("wi", s), ("hT", t)], w=[ppk])
                            eng = "dve" if t % 2 == 0 else "act"
                            if eng == "dve":
                                S.add("dve", lambda e, t=t, pp=pp, sv=sv: e.tensor_copy(out=sv[:, t, :], in_=pp[:, 0:256]),
                                      r=[ppk], w=[(svk, t)])
                            else:
                                S.add("act", lambda e, t=t, pp=pp, sv=sv: e.activation(out=sv[:, t, :], in_=pp[:, 0:256], func=AF.Copy),
                                      r=[ppk], w=[(svk, t)])
                        if g == 17:
                            dst = vs[r0:r0 + TB, :].rearrange("(t p) f -> p t f", p=128)
                        else:
                            dst = vd[r0:r0 + TB, (g - 8) * 256:(g - 7) * 256].rearrange("(t p) f -> p t f", p=128)
                        S.add("sp", lambda e, dst=dst, sv=sv: e.dma_start(out=dst, in_=sv),
                              r=[(svk, t) for t in range(NT)], dma="stv%d" % svk[1])
                    else:
                        for c2 in range(2):
                            col = col0 + c2 * 128
                            isgate = col >= 4608
                            stg = (sgf if isgate else sgb)[ccnt % 2]
                            stk = ("sgf" if isgate else "sgb", ccnt % 2)
                            ccnt += 1
                            for sub in range(TB // 512):
                                pp = pq[pcnt % 4]
                                ppk = ("pq", pcnt % 4)
                                pcnt += 1
                                tk = [("hT", t) for t in range(sub * 4, sub * 4 + 4)]
                                for kc in range(8):
                                    S.add("pe", lambda e, kc=kc, s=s, c2=c2, sub=sub, pp=pp: e.matmul(
                                        pp, lhsT=wi[s][:, kc, c2 * 128:(c2 + 1) * 128], rhs=hT[:, kc, sub * 512:(sub + 1) * 512],
                                        start=(kc == 0), stop=(kc == 7)), r=[("wi", s)] + tk, w=[ppk])
                                if isgate:
                                    gc = (col - 4608) // 128
                                    S.add("act", lambda e, pp=pp, stg=stg, sub=sub, gc=gc: e.activation(
                                        out=stg[:, sub * 512:(sub + 1) * 512], in_=pp, func=AF.Sigmoid, bias=bgT[:, gc:gc + 1], scale=1.0),
                                        r=[ppk, "bgT"], w=[(stk, sub)])
                                elif sub % 2 == 0:
                                    S.add("dve", lambda e, pp=pp, stg=stg, sub=sub: e.tensor_copy(out=stg[:, sub * 512:(sub + 1) * 512], in_=pp),
                                          r=[ppk], w=[(stk, sub)])
                                else:
                                    S.add("act", lambda e, pp=pp, stg=stg, sub=sub: e.activation(out=stg[:, sub * 512:(sub + 1) * 512], in_=pp, func=AF.Copy),
                                          r=[ppk], w=[(stk, sub)])
                            if col < 1024:
                                dst = qTd[col:col + 128, r0:r0 + TB]
                            elif col < 2048:
                                dst = kTd[col - 1024:col - 1024 + 128, r0:r0 + TB]
                            elif col < 4096:
                                dst = qTs[col - 3072:col - 3072 + 128, r0:r0 + TB]
                            elif col < 4352:
                                dst = kTs[col - 4096:col - 4096 + 128, r0:r0 + TB]
                            else:
                                dst = gTs[col - 4608:col - 4608 + 128, r0:r0 + TB]
                            S.add("sp", lambda e, dst=dst, stg=stg: e.dma_start(out=dst, in_=stg),
                                  r=[(stk, sub) for sub in range(TB // 512)], dma="st%s%d" % (stk[0], stk[1]))
                    if g + 3 < NG:
                        load_wi(g + 3)
                S.barrier()

    def store_oT(otok, dstT, c0, tagp, barrier=True):
        with ExitStack() as _st6:
            ost0 = _st6.enter_context(sb("ost0", [128, 8, 512], BF16))
            ost1 = _st6.enter_context(sb("ost1", [128, 8, 512], BF16))
            pot0 = _st6.enter_context(pst("pot0", [128, 1024], BF16))
            pot1 = _st6.enter_context(pst("pot1", [128, 1024], BF16))
            ost = [ost0.ap(), ost1.ap()]
            pot = [pot0.ap(), pot1.ap()]
            for qb in range(4):
                st = ost[qb % 2]
                for qi in range(4):
                    n = qb * 4 + qi
                    pp = pot[n % 2]
                    for kc in range(8):
                        S.add("pe", lambda e, n=n, kc=kc, pp=pp: e.transpose(out=pp[:, kc * 128:(kc + 1) * 128],
                                                                           in_=otok[:, n, kc * 128:(kc + 1) * 128], identity=ident),
                              r=[("otok", n), "ident"], w=[("pot", n % 2)])
                    if n % 2 == 0:
                        S.add("act", lambda e, pp=pp, st=st, qi=qi: e.activation(out=st[:, :, qi * 128:(qi + 1) * 128],
                                                                                in_=pp.rearrange("p (k t) -> p k t", k=8), func=AF.Copy),
                              r=[("pot", n % 2)], w=[("ost", qb % 2, qi)])
                    else:
                        S.add("dve", lambda e, pp=pp, st=st, qi=qi: e.tensor_copy(out=st[:, :, qi * 128:(qi + 1) * 128],
                                                                                 in_=pp.rearrange("p (k t) -> p k t", k=8)),
                              r=[("pot", n % 2)], w=[("ost", qb % 2, qi)])
                S.add("sp", lambda e, st=st, qb=qb: e.dma_start(
                    out=dstT[:, c0 + qb * 512:c0 + (qb + 1) * 512].rearrange("(k p) t -> p k t", p=128), in_=st),
                    r=[("ost", qb % 2, qi) for qi in range(4)], dma="sto%d" % (qb % 2))
            if barrier:
                S.barrier()

    def phase_B1(seq, otok):
        c0 = seq * SEQ
        NM = 3968
        if True:
            with ExitStack() as _st8:
                qT_t = _st8.enter_context(sb("qT", [128, 2, SEQ], BF16))
                kT_t = _st8.enter_context(sb("kT", [128, 2, SEQ], BF16))
                td0 = _st8.enter_context(sb("td0", [128, NM], BF16))
                td1 = _st8.enter_context(sb("td1", [128, NM], BF16))
                erl = [_st8.enter_context(sb("er%d" % i_, [128, 768], BF16)) for i_ in range(3)]
                va_t = _st8.enter_context(sb("vaug", [128, 16, 8, 129], BF16))
                dd_t = _st8.enter_context(sb("dd", [128, NM], F32))
                mhi_t = _st8.enter_context(sb("mhi", [128, NM], BF16))
                mlo_t = _st8.enter_context(sb("mlo", [128, NM], BF16))
                ih_t = _st8.enter_context(sb("ih", [128, 8, 128], BF16))
                etl = [_st8.enter_context(sb("et%d" % i_, [128, 768], BF16)) for i_ in range(7)]
                rd_t = _st8.enter_context(sb("rd", [128, 2, 4], F32))
                rl2_t = _st8.enter_context(sb("rl2", [128, 4], F32))
                t1_t = _st8.enter_context(sb("t1", [128, 3, 128], F32))
                of_t = _st8.enter_context(sb("of", [128, 3, 128], F32))
                jk2_t = _st8.enter_context(sb("jk2", [128, 128], BF16))
                jk2f_t = _st8.enter_context(sb("jk2f", [128, 128], F32))
                ss2_t = _st8.enter_context(sb("ss2", [128, 4], F32))
                ln2_t = _st8.enter_context(sb("ln2", [128, 4], F32))
                rs2_t = _st8.enter_context(sb("rs2", [128, 4], F32))
                pcl = [_st8.enter_context(sb("pc%d" % i_, [128, 387], F32)) for i_ in range(2)]
                psl = [_st8.enter_context(pst("ps%d" % i_, [128, 1024], F32)) for i_ in range(3)]
                pol = [_st8.enter_context(pst("po%d" % i_, [128, 512], F32)) for i_ in range(2)]
                qT, kT, vaug = qT_t.ap(), kT_t.ap(), va_t.ap()
                dd = dd_t.ap()
                Mhi, Mlo, Ih = mhi_t.ap(), mlo_t.ap(), ih_t.ap()
                td = [td0.ap(), td1.ap()]
                er = [t_.ap() for t_ in erl]
                et = [t_.ap() for t_ in etl]
                rd, rl2, t1, of = rd_t.ap(), rl2_t.ap(), t1_t.ap(), of_t.ap()
                jk2, ss2, ln2, rs2 = jk2_t.ap(), ss2_t.ap(), ln2_t.ap(), rs2_t.ap()
                jk2f = jk2f_t.ap()
                ps = [t_.ap() for t_ in psl]
                po = [t_.ap() for t_ in pol]
                pc = [t_.ap() for t_ in pcl]
                def ld_qk(h):
                    sl_ = h % 2
                    S.add("sp", lambda e: e.dma_start(out=qT[:, sl_, :], in_=qTd[h * 128:(h + 1) * 128, c0:c0 + SEQ]),
                          w=[("qk", sl_)], dma="bqk%d" % sl_)
                    S.add("sp", lambda e: e.dma_start(out=kT[:, sl_, :], in_=kTd[h * 128:(h + 1) * 128, c0:c0 + SEQ]),
                          w=[("qk", sl_)], dma="bqk%d" % sl_)
                ld_qk(0)
                S.add("pool", lambda e: e.memset(vaug[:, :, :, 128:129], 1.0), w=["vones"])
                for kc in range(16):
                    S.add("sp", lambda e, kc=kc: e.dma_start(
                        out=vaug[:, kc, :, 0:128],
                        in_=vd[c0 + kc * 128:c0 + (kc + 1) * 128, :].rearrange("p (h e) -> p h e", h=8)),
                        w=[("v", kc)], dma="bv")
                ld_qk(1)
                S.add("pool", lambda e: e.iota(dd, [[1, NM]], base=-1920, channel_multiplier=-1,
                                               allow_small_or_imprecise_dtypes=True), w=["dd"])
                S.add("act", lambda e: e.activation(out=dd, in_=dd, func=AF.Abs), r=["dd"], w=["dd"])
                S.add("act", lambda e: e.activation(out=Mhi, in_=dd, func=AF.Copy, scale=-1.0), r=["dd"], w=["Mhi"])
                S.add("dve", lambda e: e.scalar_tensor_tensor(out=Mlo, in0=dd, scalar=-1.0, in1=Mhi, op0=ALU.mult, op1=ALU.subtract),
                      r=["dd", "Mhi"], w=["Mlo"])
                for h in range(8):
                    S.add("dve", lambda e, h=h: e.tensor_scalar(out=Ih[:, h, :], in0=ident, scalar1=2.0 ** (2 - h), scalar2=None, op0=ALU.mult),
                          r=["ident"], w=[("Ih", h)])
                qblocks = [(0, 3), (3, 3), (6, 3), (9, 3), (12, 2), (14, 2)]
                steps = [(h, bi_, kc) for h in range(8) for bi_ in range(len(qblocks)) for kc in range(16)]
                NS = len(steps)
                LAG = 5
                NPS = 3
                NB = 3
                NBE = 7

                TDP = NM // 8

                def gen_td(h, piece):
                    sl = 2.0 ** (-(h + 1))
                    tdh = td[h % 2]
                    S.add("act", lambda e: e.activation(out=tdh[:, piece * TDP:(piece + 1) * TDP],
                                                        in_=dd[:, piece * TDP:(piece + 1) * TDP], func=AF.Exp, scale=-sl),
                          r=["dd"], w=[("td", h % 2, piece)])

                def b1_qk(i):
                    h, bi_, kc = steps[i]
                    t0_, nq = qblocks[bi_]
                    q0, k0, Wd = t0_ * 128, kc * 128, nq * 128
                    slot = i % NPS
                    pt_ = ps[slot]
                    nbuf = i % NBE
                    nbr = i % NB
                    hs = h % 2
                    m0 = q0 - k0 + 1920
                    tdh = td[h % 2]
                    use_lo = h >= 4
                    on_pe = (i % 5 == 2) if use_lo else (i % 5 in (1, 3))
                    if kc == 0 and h + 1 < 8:
                        gen_td(h + 1, bi_)
                        if bi_ == 5:
                            gen_td(h + 1, 6)
                            gen_td(h + 1, 7)
                        if bi_ == 0 and h >= 1:
                            ld_qk(h + 1)
                    S.add("pe", lambda e: e.matmul(
                        pt_[:, 0:Wd], lhsT=kT[0:64, hs, k0:k0 + 128], rhs=qT[0:64, hs, q0:q0 + Wd],
                        start=True, stop=(not on_pe)), r=[("qk", hs)], w=[("ps", slot, 0)])
                    S.add("pe", lambda e: e.matmul(
                        pt_[:, 512:512 + Wd], lhsT=kT[64:128, hs, k0:k0 + 128], rhs=qT[64:128, hs, q0:q0 + Wd],
                        start=True, stop=(not on_pe)), r=[("qk", hs)], w=[("ps", slot, 1)])
                    if on_pe:
                        for comp in range(2):
                            S.add("pe", lambda e, comp=comp: e.matmul(
                                pt_[:, comp * 512:comp * 512 + Wd], lhsT=Ih[:, h, :], rhs=Mhi[:, m0:m0 + Wd],
                                start=False, stop=(not use_lo)), r=[("Ih", h), "Mhi"], w=[("ps", slot, comp)])
                        if use_lo:
                            for comp in range(2):
                                S.add("pe", lambda e, comp=comp: e.matmul(
                                    pt_[:, comp * 512:comp * 512 + Wd], lhsT=Ih[:, h, :], rhs=Mlo[:, m0:m0 + Wd],
                                    start=False, stop=True), r=[("Ih", h), "Mlo"], w=[("ps", slot, comp)])
                        for comp in range(2):
                            S.add("act", lambda e, comp=comp: e.activation(
                                out=et[nbuf][:, comp * 384:comp * 384 + Wd],
                                in_=pt_[:, comp * 512:comp * 512 + Wd], func=AF.Exp, scale=0.125),
                                r=[("ps", slot, comp)], w=[("et", nbuf, comp)])
                    else:
                        for comp in range(2):
                            S.add("act", lambda e, comp=comp: e.activation(
                                out=er[nbr][:, comp * 384:comp * 384 + Wd],
                                in_=pt_[:, comp * 512:comp * 512 + Wd], func=AF.Exp, scale=0.125),
                                r=[("ps", slot, comp)], w=[("er", nbr, comp)])
                            S.add("dve", lambda e, comp=comp: e.tensor_tensor(
                                out=et[nbuf][:, comp * 384:comp * 384 + Wd], in0=er[nbr][:, comp * 384:comp * 384 + Wd],
                                in1=tdh[:, m0:m0 + Wd], op=ALU.mult),
                                r=[("er", nbr, comp)] + [("td", h % 2, p_) for p_ in range(8)], w=[("et", nbuf, comp)])

                def b1_av(i):
                    h, bi_, kc = steps[i]
                    t0_, nq = qblocks[bi_]
                    nbuf = i % NBE
                    for comp in range(2):
                        for qi in range(nq):
                            S.add("pe", lambda e, qi=qi, comp=comp: e.matmul(
                                po[comp][:, qi * 129:(qi + 1) * 129],
                                lhsT=et[nbuf][:, comp * 384 + qi * 128:comp * 384 + (qi + 1) * 128],
                                rhs=vaug[:, kc, h, :], start=(kc == 0 and qi == 0), stop=(kc == 15 and qi == nq - 1)),
                                r=[("et", nbuf, comp), ("v", kc), "vones"], w=[("po", comp)])
                    if kc != 15:
                        return
                    S.add("dve", lambda e: e.tensor_copy(out=pc[0][:, 0:nq * 129], in_=po[0][:, 0:nq * 129]), r=[("po", 0)], w=[("pc", 0)])
                    S.add("dve", lambda e: e.tensor_copy(out=pc[1][:, 0:nq * 129], in_=po[1][:, 0:nq * 129]), r=[("po", 1)], w=[("pc", 1)])
                    pv0 = pc[0].rearrange("p (q e) -> p q e", e=129)
                    pv1 = pc[1].rearrange("p (q e) -> p q e", e=129)
                    S.add("dve", lambda e: e.reciprocal(out=rd[:, 0, 0:nq], in_=pv0[:, 0:nq, 128]), r=[("pc", 0)], w=["rd0"])
                    S.add("dve", lambda e: e.reciprocal(out=rd[:, 1, 0:nq], in_=pv1[:, 0:nq, 128]), r=[("pc", 1)], w=["rd1"])
                    S.add("dve", lambda e: e.tensor_scalar(out=rl2[:, 0:nq], in0=rd[:, 1, 0:nq], scalar1=neglam, scalar2=None, op0=ALU.mult),
                          r=["rd1", "neglam"], w=["rl2"])
                    for qi in range(nq):
                        S.add("dve", lambda e, qi=qi: e.tensor_scalar(out=t1[:, qi, :], in0=pv0[:, qi, 0:128], scalar1=rd[:, 0, qi:qi + 1],
                                                                      scalar2=None, op0=ALU.mult),
                              r=[("pc", 0), "rd0"], w=[("t1", qi)])
                        S.add("dve", lambda e, qi=qi: e.scalar_tensor_tensor(out=of[:, qi, :], in0=pv1[:, qi, 0:128], scalar=rl2[:, qi:qi + 1],
                                                                             in1=t1[:, qi, :], op0=ALU.mult, op1=ALU.add),
                              r=[("pc", 1), "rl2", ("t1", qi)], w=[("of", qi)])

                    def part_b(nq=nq):
                        for qi in range(nq):
                            S.add("act", lambda e, qi=qi: e.activation(out=jk2, in_=of[:, qi, :], func=AF.Square, scale=128.0 ** -0.5,
                                                                       accum_out=ss2[:, qi:qi + 1]),
                                  r=[("of", qi)], w=["jk2", ("ss2", qi)])
                        S.add("act", lambda e: e.activation(out=ln2[:, 0:nq], in_=ss2[:, 0:nq], func=AF.Ln, bias=epsb, scale=1.0),
                              r=[("ss2", qi) for qi in range(nq)], w=["ln2"])
                        S.add("act", lambda e: e.activation(out=rs2[:, 0:nq], in_=ln2[:, 0:nq], func=AF.Exp, scale=-0.5),
                              r=["ln2"], w=["rs2"])

                    def part_c(nq=nq, t0_=t0_, h=h):
                        for qi in range(nq):
                            n = t0_ + qi
                            S.add("dve", lambda e, qi=qi, n=n: e.scalar_tensor_tensor(
                                out=otok[:, n, h * 128:(h + 1) * 128], in0=of[:, qi, :], scalar=rs2[:, qi:qi + 1], in1=gsub,
                                op0=ALU.mult, op1=ALU.mult), r=[("of", qi), "rs2", "gsub"], w=[("otok", n, h)])
                    deferred.setdefault(i + 2, []).append(part_b)
                    deferred.setdefault(i + 5, []).append(part_c)

                deferred = {}
                for p_ in range(8):
                    gen_td(0, p_)
                for j in range(NS + LAG):
                    if j < NS:
                        b1_qk(j)
                    i = j - LAG
                    if i >= 0:
                        b1_av(i)
                        for f_ in deferred.pop(i, []):
                            f_()
                for k_ in sorted(deferred):
                    for f_ in deferred[k_]:
                        f_()
                S.barrier()
            pass

    def phase_B2(seq, pre=None):
        c0 = seq * SEQ
        with ExitStack() as _st9:
            otok_t = _st9.enter_context(sb("otok2", [128, 16, D], BF16))
            otok = otok_t.ap()
            with ExitStack() as _st10:
                q_t = _st10.enter_context(sb("qsw", [128, 8, SEQ], BF16))
                k_t = _st10.enter_context(sb("ksw", [128, 4, SEQ], BF16))
                v_t = _st10.enter_context(sb("vsw", [128, 16, 4, 65], BF16))
                er0 = _st10.enter_context(sb("er0", [128, 3, 512], BF16))
                er1 = _st10.enter_context(sb("er1", [128, 3, 512], BF16))
                et0 = _st10.enter_context(sb("et0", [128, 3, 512], BF16))
                et1 = _st10.enter_context(sb("et1", [128, 3, 512], BF16))
                dn0 = _st10.enter_context(sb("dn0", [128, 4], F32))
                dn1 = _st10.enter_context(sb("dn1", [128, 4], F32))
                rn0 = _st10.enter_context(sb("rn0", [128, 4], F32))
                rn1 = _st10.enter_context(sb("rn1", [128, 4], F32))
                pssl = [[_st10.enter_context(pst("pss%d%d" % (a_, b_), [128, 512], F32)) for b_ in range(2)] for a_ in range(2)]
                pos0 = _st10.enter_context(pst("pos0", [128, 512], F32))
                pos1 = _st10.enter_context(pst("pos1", [128, 512], F32))
                qsw, ksw, vsw = q_t.ap(), k_t.ap(), v_t.ap()
                er = [er0.ap(), er1.ap()]
                et = [et0.ap(), et1.ap()]
                dn = [dn0.ap(), dn1.ap()]
                rn = [rn0.ap(), rn1.ap()]
                pss = [[t_.ap() for t_ in row_] for row_ in pssl]
                pos = [pos0.ap(), pos1.ap()]
                for g in range(4):
                    for hf in range(2):
                        rw = (4 * g + 2 * hf) * 64
                        S.add("sp", lambda e, g=g, hf=hf, rw=rw: e.dma_start(
                            out=qsw[hf * 64:(hf + 1) * 64, 2 * g:2 * g + 2, :],
                            in_=qTs[rw:rw + 128, c0:c0 + SEQ].rearrange("(j d) t -> d j t", d=64)),
                            w=[("qsw", g, hf)], dma="b2l")
                for hf in range(2):
                    S.add("sp", lambda e, hf=hf: e.dma_start(
                        out=ksw[hf * 64:(hf + 1) * 64, :, :],
                        in_=kTs[:, c0:c0 + SEQ].rearrange("(g d) t -> d g t", d=64)), w=[("ksw", hf)], dma="b2l")
                S.add("pool", lambda e: e.memset(vsw[:, :, :, 64:65], 1.0), w=["vones2"])
                for kc in range(16):
                    S.add("sp", lambda e, kc=kc: e.dma_start(
                        out=vsw[:, kc, :, 0:64],
                        in_=vs[c0 + kc * 128:c0 + (kc + 1) * 128, :].rearrange("p (g e) -> p g e", g=4)),
                        w=[("vsw", kc // 4)], dma="b2l")
                if pre is not None:
                    pre()
                steps2 = [(n, g) for n in range(16) for g in range(4)]

                def b2_qk(i):
                    n, g = steps2[i]
                    par = i % 2
                    blocks = [j for j in (n - 1, n, n + 1) if 0 <= j < 16]
                    nb = len(blocks)
                    for hf in range(2):
                        for bk in range(2):
                            bis = [bi for bi in range(nb) if (bi // 2) == bk]
                            if not bis:
                                continue
                            for bi in bis:
                                j = blocks[bi]
                                for jj in range(2):
                                    first = (bi == bis[0] and jj == 0)
                                    last = (bi == bis[-1] and jj == 1)
                                    cc = (bi % 2) * 256 + jj * 128
                                    S.add("pe", lambda e, bk=bk, j=j, hf=hf, jj=jj, cc=cc, first=first, last=last: e.matmul(
                                        pss[hf][bk][:, cc:cc + 128],
                                        lhsT=ksw[hf * 64:(hf + 1) * 64, g, j * 128:(j + 1) * 128],
                                        rhs=qsw[hf * 64:(hf + 1) * 64, 2 * g + jj, n * 128:(n + 1) * 128],
                                        start=first, stop=last),
                                        r=[("qsw", g, hf), ("ksw", hf)], w=[("pss", hf, bk)])
                            for bi in bis:
                                cc = (bi % 2) * 256
                                S.add("act", lambda e, bi=bi, hf=hf, bk=bk, cc=cc: e.activation(
                                    out=er[par][:, bi, hf * 256:(hf + 1) * 256], in_=pss[hf][bk][:, cc:cc + 256],
                                    func=AF.Exp, scale=0.125),
                                    r=[("pss", hf, bk)], w=[("er", par, bi, hf)])
                    for bi, j in enumerate(blocks):
                        ri = 1 - (j - n)
                        S.add("dve", lambda e, bi=bi, ri=ri: e.tensor_tensor(
                            out=et[par][:, bi, :].rearrange("p (h q) -> p h q", h=4),
                            in0=er[par][:, bi, :].rearrange("p (h q) -> p h q", h=4),
                            in1=Tsw[:, 4 * g:4 * g + 4, ri * 128:(ri + 1) * 128], op=ALU.mult),
                            r=[("er", par, bi, 0), ("er", par, bi, 1)], w=[("et", par, bi)])

                def b2_av(i):
                    n, g = steps2[i]
                    par = i % 2
                    blocks = [j for j in (n - 1, n, n + 1) if 0 <= j < 16]
                    nb = len(blocks)
                    for hh in range(4):
                        for bi, j in enumerate(blocks):
                            S.add("pe", lambda e, hh=hh, bi=bi, j=j: e.matmul(
                                pos[par][:, hh * 65:(hh + 1) * 65], lhsT=et[par][:, bi, hh * 128:(hh + 1) * 128],
                                rhs=vsw[:, j, g, :], start=(bi == 0 and hh == 0), stop=(bi == nb - 1 and hh == 3)),
                                r=[("et", par, bi), ("vsw", j // 4), "vones2"], w=[("pos", par)])
                    pv = pos[par][:, 0:260].rearrange("p (h e) -> p h e", h=4)
                    S.add("dve", lambda e: e.tensor_tensor(
                        out=dn[par], in0=pv[:, :, 64], in1=es[:, 4 * g:4 * g + 4], op=ALU.add),
                        r=[("pos", par), "es"], w=[("dn", par)])
                    S.add("dve", lambda e: e.reciprocal(out=rn[par], in_=dn[par]), r=[("dn", par)], w=[("rn", par)])
                    S.add("dve", lambda e: e.tensor_tensor(
                        out=otok[:, n, 4 * g * 64:(4 * g + 4) * 64].rearrange("p (h e) -> p h e", h=4),
                        in0=pv[:, :, 0:64], in1=rn[par].unsqueeze(2).broadcast_to([128, 4, 64]), op=ALU.mult),
                        r=[("pos", par), ("rn", par)], w=[("otok", n, g)])

                b2_qk(0)
                for i in range(len(steps2)):
                    if i + 1 < len(steps2):
                        b2_qk(i + 1)
                    b2_av(i)
                S.barrier()
            store_oT(otok, oTs, c0, "s")

    def phase_C(blk):
        r0 = blk * TB
        with ExitStack() as _st11:
            xres_t = _st11.enter_context(sb("xres", [128, NT, D], F32))
            hT_t = _st11.enter_context(sb("hT", [128, 8, TB], BF16))
            hb0 = _st11.enter_context(sb("hb0", [128, D], BF16))
            hb1 = _st11.enter_context(sb("hb1", [128, D], BF16))
            junk_t = _st11.enter_context(sb("junk", [128, D], BF16))
            ssq_t = _st11.enter_context(sb("ssq", [128, NT], F32))
            lnv_t = _st11.enter_context(sb("lnv", [128, NT], F32))
            rstd_t = _st11.enter_context(sb("rstd", [128, NT], F32))
            xres, hT = xres_t.ap(), hT_t.ap()
            hb = [hb0.ap(), hb1.ap()]
            junk, ssq, lnv, rstd = junk_t.ap(), ssq_t.ap(), lnv_t.ap(), rstd_t.ap()
            with ExitStack() as _st12:
                oTa_t = _st12.enter_context(sb("oTa", [128, 8, TB], BF16))
                oTb_t = _st12.enter_context(sb("oTb", [128, 8, TB], BF16))
                PA_t = _st12.enter_context(sb("PA", [128, 8, D], BF16))
                PB_t = _st12.enter_context(sb("PB", [128, 8, D], BF16))
                WO_t = _st12.enter_context(sb("WO", [128, 8, D], BF16))
                mT_t = _st12.enter_context(sb("mT", [128, 8, TB], BF16))
                ga0 = _st12.enter_context(sb("ga0", [128, TB], F32))
                ga1 = _st12.enter_context(sb("ga1", [128, TB], F32))
                gb0 = _st12.enter_context(sb("gb0", [128, TB], F32))
                gb1 = _st12.enter_context(sb("gb1", [128, TB], F32))
                m10 = _st12.enter_context(sb("m10", [128, 512], F32))
                m11 = _st12.enter_context(sb("m11", [128, 512], F32))
                m20 = _st12.enter_context(sb("m20", [128, 512], F32))
                m21 = _st12.enter_context(sb("m21", [128, 512], F32))
                p0 = _st12.enter_context(pst("ptr0", [128, 1024], BF16))
                p1 = _st12.enter_context(pst("ptr1", [128, 1024], BF16))
                pa0 = _st12.enter_context(pst("pa0", [128, 512], F32))
                pa1 = _st12.enter_context(pst("pa1", [128, 512], F32))
                pb0 = _st12.enter_context(pst("pb0", [128, 512], F32))
                pb1 = _st12.enter_context(pst("pb1", [128, 512], F32))
                py0 = _st12.enter_context(pst("py0", [128, 512], F32))
                py1 = _st12.enter_context(pst("py1", [128, 512], F32))
                oTa, oTb, PA, PB, WO, mT = oTa_t.ap(), oTb_t.ap(), PA_t.ap(), PB_t.ap(), WO_t.ap(), mT_t.ap()
                ga = [ga0.ap(), ga1.ap()]
                gbt = [gb0.ap(), gb1.ap()]
                m1 = [m10.ap(), m11.ap()]
                m2 = [m20.ap(), m21.ap()]
                ptr = [p0.ap(), p1.ap()]
                pa = [pa0.ap(), pa1.ap()]
                pb = [pb0.ap(), pb1.ap()]
                py = [py0.ap(), py1.ap()]
                S.add("sp", lambda e: e.dma_start(out=oTa, in_=oTd[:, r0:r0 + TB].rearrange("(k p) t -> p k t", p=128)), w=["oTa"], dma="co")
                S.add("sp", lambda e: e.dma_start(out=oTb, in_=oTs[:, r0:r0 + TB].rearrange("(k p) t -> p k t", p=128)), w=["oTb"], dma="co")
                for hlf in range(2):
                    S.add("pool", lambda e, hlf=hlf: e.dma_start(out=PA[:, hlf * 4:(hlf + 1) * 4, :],
                                                                in_=W["w_proj_da"][hlf * 512:(hlf + 1) * 512, :].rearrange("(k p) f -> p k f", p=128)),
                          w=[("PA", hlf)], dma="cw0")
                    S.add("pool", lambda e, hlf=hlf: e.dma_start(out=PB[:, hlf * 4:(hlf + 1) * 4, :],
                                                                in_=W["w_proj_swa"][hlf * 512:(hlf + 1) * 512, :].rearrange("(k p) f -> p k f", p=128)),
                          w=[("PB", hlf)], dma="cw0")
                for hlf in range(2):
                    S.add("pool", lambda e, hlf=hlf: e.dma_start(out=WO[:, hlf * 4:(hlf + 1) * 4, :],
                                                                in_=W["w_out"][hlf * 512:(hlf + 1) * 512, :].rearrange("(k p) f -> p k f", p=128)),
                          w=[("WO", hlf)], dma="cw1")

                def load_g(c):
                    s = c % 2
                    S.add("sp", lambda e: e.dma_start(out=ga[s], in_=gTs[c * 128:(c + 1) * 128, r0:r0 + TB]), w=[("ga", s)], dma="cga%d" % s)
                    S.add("sp", lambda e: e.dma_start(out=gbt[s], in_=gTs[1024 + c * 128:1024 + (c + 1) * 128, r0:r0 + TB]), w=[("gbt", s)], dma="cgb%d" % s)
                load_g(0)
                load_g(1)
                for t in range(NT):
                    S.add("sp", lambda e, t=t: e.dma_start(out=xres[:, t, :], in_=x1s[r0 + t * 128:r0 + (t + 1) * 128, :]),
                          w=[("x", t)], dma="x%d" % t)
                cnt = 0
                for c in range(8):
                    s = c % 2
                    for sub in range(TB // 512):
                        par = cnt % 2
                        cnt += 1
                        for k in range(8):
                            S.add("pe", lambda e, k=k, c=c, sub=sub, par=par: e.matmul(
                                pa[par], lhsT=PA[:, k, c * 128:(c + 1) * 128], rhs=oTa[:, k, sub * 512:(sub + 1) * 512],
                                start=(k == 0), stop=(k == 7)), r=[("PA", k // 4), "oTa"], w=[("pa", par)])
                        for k in range(8):
                            S.add("pe", lambda e, k=k, c=c, sub=sub, par=par: e.matmul(
                                pb[par], lhsT=PB[:, k, c * 128:(c + 1) * 128], rhs=oTb[:, k, sub * 512:(sub + 1) * 512],
                                start=(k == 0), stop=(k == 7)), r=[("PB", k // 4), "oTb"], w=[("pb", par)])
                        S.add("dve", lambda e, par=par, s=s, sub=sub: e.tensor_tensor(
                            out=m1[par], in0=pa[par], in1=ga[s][:, sub * 512:(sub + 1) * 512], op=ALU.mult),
                            r=[("pa", par), ("ga", s)], w=[("m1", par)])
                        S.add("dve", lambda e, par=par, s=s, sub=sub: e.tensor_tensor(
                            out=m2[par], in0=pb[par], in1=gbt[s][:, sub * 512:(sub + 1) * 512], op=ALU.mult),
                            r=[("pb", par), ("gbt", s)], w=[("m2", par)])
                        S.add("pool", lambda e, par=par, c=c, sub=sub: e.tensor_tensor(
                            out=mT[:, c, sub * 512:(sub + 1) * 512], in0=m1[par], in1=m2[par], op=ALU.add),
                            r=[("m1", par), ("m2", par)], w=[("mT", c, sub)])
                    if c + 2 < 8:
                        load_g(c + 2)
                cnt = 0
                for t in range(NT):
                    for half in range(2):
                        par = cnt % 2
                        cnt += 1
                        for c in range(8):
                            S.add("pe", lambda e, t=t, half=half, c=c, par=par: e.matmul(
                                py[par], lhsT=mT[:, c, t * 128:(t + 1) * 128], rhs=WO[:, c, half * 512:(half + 1) * 512],
                                start=(c == 0), stop=(c == 7)), r=[("mT", c, t // 4), ("WO", c // 4)], w=[("py", par)])
                        S.add("dve", lambda e, t=t, half=half, par=par: e.tensor_tensor(
                            out=xres[:, t, half * 512:(half + 1) * 512], in0=py[par], in1=xres[:, t, half * 512:(half + 1) * 512], op=ALU.add),
                            r=[("py", par), ("x", t)], w=[("x", t)])
                norm_to_hT(xres, 2, hT, hb, ptr, junk, ssq, lnv, rstd)
                S.barrier()
            with ExitStack() as _st13:
                pg0 = _st13.enter_context(pst("pg0", [128, 512], F32))
                pg1 = _st13.enter_context(pst("pg1", [128, 512], F32))
                pu0 = _st13.enter_context(pst("pu0", [128, 512], F32))
                pu1 = _st13.enter_context(pst("pu1", [128, 512], F32))
                py0 = _st13.enter_context(pst("py0", [128, 512], F32))
                py1 = _st13.enter_context(pst("py1", [128, 512], F32))
                ofl = [_st13.enter_context(sb("of%d" % i_, [128, D], F32)) for i_ in range(3)]
                pg = [pg0.ap(), pg1.ap()]
                pu = [pu0.ap(), pu1.ap()]
                py = [py0.ap(), py1.ap()]
                ofb = [t_.ap() for t_ in ofl]
                ffn(xres, hT, W["ffn2_gate"], W["ffn2_up"], W["ffn2_down"], pg, pu, py)
                def fstats(t):
                    S.add("act", lambda e: e.activation(out=junk, in_=xres[:, t, :], func=AF.Square, scale=1.0 / 32.0,
                                                        accum_out=ssq[:, t:t + 1]), r=[("x", t)], w=["junk", ("ssq", t)])
                    S.add("act", lambda e: e.activation(out=lnv[:, t:t + 1], in_=ssq[:, t:t + 1], func=AF.Ln, bias=epsb, scale=1.0),
                          r=[("ssq", t)], w=[("lnv", t)])
                    S.add("act", lambda e: e.activation(out=rstd[:, t:t + 1], in_=lnv[:, t:t + 1], func=AF.Exp, scale=-0.5),
                          r=[("lnv", t)], w=[("rstd", t)])
                for t in range(NT):
                    fstats(t)
                    S.add("dve", lambda e, t=t: e.scalar_tensor_tensor(out=ofb[t % 3], in0=xres[:, t, :], scalar=rstd[:, t:t + 1],
                                                                     in1=gb[:, 3, :], op0=ALU.mult, op1=ALU.mult),
                          r=[("x", t), ("rstd", t), ("gb", 3)], w=[("ofb", t % 3)])
                    S.add("sp", lambda e, t=t: e.dma_start(out=out[r0 + t * 128:r0 + (t + 1) * 128, :], in_=ofb[t % 3]),
                          r=[("ofb", t % 3)], dma="sto%d" % (t % 3))
                S.barrier()

    setup()
    for seq in range(nseq):
        if "A" in phases:
            phase_A(2 * seq)
            phase_A(2 * seq + 1)
        if "B" in phases:
            with ExitStack() as _stb:
                otok1 = _stb.enter_context(sb("otok", [128, 16, D], BF16)).ap()
                phase_B1(seq, otok1)
                phase_B2(seq, pre=lambda: store_oT(otok1, oTd, seq * SEQ, "d", barrier=False))
        else:
            if "B1" in phases:
                with ExitStack() as _stb:
                    otok1 = _stb.enter_context(sb("otok", [128, 16, D], BF16)).ap()
                    phase_B1(seq, otok1)
                    store_oT(otok1, oTd, seq * SEQ, "d")
            if "B2" in phases:
                phase_B2(seq)
        if "C" in phases:
            phase_C(2 * seq)
            phase_C(2 * seq + 1)
    S.finish()
    return nc


_WNAMES = ["norm_ffn1", "ffn1_gate", "ffn1_up", "ffn1_down", "norm_mix", "w_in", "b_gate", "da_lambda",
           "da_subnorm", "swa_sink", "w_proj_da", "w_proj_swa", "w_out", "norm_ffn2", "ffn2_gate", "ffn2_up",
           "ffn2_down", "norm_final"]


def prep_weights(inputs):
    w = {}
    for nm in _WNAMES:
        a = np.asarray(inputs[nm], dtype=np.float32)
        if nm != "norm_final":
            a = a[0]
        if nm in ("b_gate", "da_lambda"):
            a = a.reshape(-1)
        w[nm] = np.ascontiguousarray(a)
    return w


def kernel(**inputs):
    x = np.asarray(inputs["x"], dtype=np.float32)
    w = prep_weights(inputs)
    nc = build_program()
    in_maps = []
    for c in range(NCORES):
        m = dict(w)
        m["x"] = np.ascontiguousarray(x[2 * c:2 * c + 2].reshape(T, D))
        in_maps.append(m)
    res = run_bass_kernel_spmd(nc, in_maps, core_ids=list(range(NCORES)))
    outs = [np.asarray(r["out"], dtype=np.float32).reshape(2, SEQ, D) for r in res.results]
    return np.concatenate(outs, axis=0)
```

```python
import numpy as np
from contextlib import ExitStack
import concourse.bass as bass
import concourse.mybir as mybir
from concourse.bass_utils import run_bass_kernel_spmd

F32 = mybir.dt.float32
BF16 = mybir.dt.bfloat16
AF = mybir.ActivationFunctionType
ALU = mybir.AluOpType
AX = mybir.AxisListType

NCORES = 8
T = 4096
SEQ = 2048
D = 1024
DFF = 2816
NFF = 22
INC = 6656
TB = 1024
NT = TB // 128


class Sched:
    ENG = ("pe", "act", "dve", "pool", "sp")

    def __init__(self, nc):
        self.nc = nc
        self.eng = {"pe": nc.tensor, "act": nc.scalar, "dve": nc.vector,
                    "pool": nc.gpsimd, "sp": nc.sync}
        self.ops = []
        self.last_w = {}
        self.readers = {}
        self.sems = {}
        self.counts = {}
        self.sig = {}
        self.waited = {e: {} for e in self.ENG}
        self.emitted = 0
        self.last_on = {}

    def add(self, eng, fn, r=(), w=(), dma=None, barrier=False):
        idx = len(self.ops)
        deps = {}
        for k in r:
            lw = self.last_w.get(k)
            if lw is not None:
                deps[lw] = True
        for k in w:
            lw = self.last_w.get(k)
            if lw is not None:
                if k in ("junk", "jk2"):
                    deps[lw] = True
                deps.setdefault(lw, False)
            for rd in self.readers.get(k, ()):
                deps.setdefault(rd, False)
        for k in r:
            self.readers.setdefault(k, []).append(idx)
        for k in w:
            self.last_w[k] = idx
            self.readers[k] = []
        deps.pop(idx, None)
        self.ops.append([eng, fn, deps, dma, barrier])
        if dma is None:
            self.last_on[eng] = idx
        return idx

    def barrier(self):
        lasts = dict(self.last_on)
        for e in self.ENG:
            idx = self.add(e, lambda en: en.nop(), barrier=True)
            for e2, li in lasts.items():
                if li >= self.emitted:
                    self.ops[idx][2][li] = True
        self.last_w = {}
        self.readers = {}
        self.emit()

    def _getsem(self, key):
        if key not in self.sems:
            self.sems[key] = self.nc.alloc_semaphore(name="s%d" % len(self.sems))
            self.counts[key] = 0
        return self.sems[key]

    def emit(self):
        ops = self.ops
        n = len(ops)
        start = self.emitted
        need = {}
        for i in range(start, n):
            eng, fn, deps, dma, bar = ops[i]
            for d, israw in deps.items():
                deng, _, _, ddma, _ = ops[d]
                if ddma is not None:
                    continue
                if deng != eng or (israw and eng != "pe"):
                    need[d] = True
        for i in range(start, n):
            eng, fn, deps, dma, bar = ops[i]
            e = self.eng[eng]
            wl = {}
            for d, israw in deps.items():
                deng, _, _, ddma, _ = ops[d]
                if ddma is not None:
                    key = ("d", ddma)
                    val = self.counts[key]
                elif deng != eng or (israw and eng != "pe"):
                    key, val = self.sig[d]
                else:
                    continue
                if wl.get(key, 0) < val:
                    wl[key] = val
            if bar:
                for key, val in self.counts.items():
                    if key[0] == "d" and val > 0:
                        wl[key] = val
            for key, val in wl.items():
                if self.waited[eng].get(key, 0) >= val:
                    continue
                self.waited[eng][key] = val
                e.wait_ge(self.sems[key], val)
            ins = fn(e)
            if dma is not None:
                key = ("d", dma)
                s = self._getsem(key)
                self.counts[key] += 16
                ins.then_inc(s, 16)
                self.sig[i] = (key, self.counts[key])
            elif need.get(i):
                key = ("e", eng)
                s = self._getsem(key)
                self.counts[key] += 1
                ins.then_inc(s, 1)
                self.sig[i] = (key, self.counts[key])
        self.emitted = n

    def finish(self):
        self.barrier()


def build_program(debug=False, phases=("A", "B", "C"), nseq=2):
    nc = bass.Bass("TRN2", target_bir_lowering=False)

    def din(name, shape, dt=F32):
        return nc.dram_tensor(name, shape, dt, kind="ExternalInput").ap()

    skind = "ExternalOutput" if debug else "Internal"

    def dscr(name, shape, dt):
        return nc.dram_tensor(name, shape, dt, kind=skind).ap()

    x_in = din("x", [T, D])
    w_names = {}
    for nm, shp in [("norm_ffn1", [D]), ("ffn1_gate", [D, DFF]), ("ffn1_up", [D, DFF]), ("ffn1_down", [DFF, D]),
                    ("norm_mix", [D]), ("w_in", [D, INC]), ("b_gate", [2 * D]), ("da_lambda", [256]),
                    ("da_subnorm", [128]), ("swa_sink", [16]), ("w_proj_da", [D, D]), ("w_proj_swa", [D, D]),
                    ("w_out", [D, D]), ("norm_ffn2", [D]), ("ffn2_gate", [D, DFF]), ("ffn2_up", [D, DFF]),
                    ("ffn2_down", [DFF, D]), ("norm_final", [D])]:
        w_names[nm] = din(nm, shp)
    W = w_names
    out = nc.dram_tensor("out", [T, D], F32, kind="ExternalOutput").ap()

    x1s = dscr("x1s", [T, D], F32)
    qTd = dscr("qTd", [D, T], BF16)
    kTd = dscr("kTd", [D, T], BF16)
    vd = dscr("vd", [T, D], BF16)
    qTs = dscr("qTs", [D, T], BF16)
    kTs = dscr("kTs", [256, T], BF16)
    vs = dscr("vs", [T, 256], BF16)
    gTs = dscr("gTs", [2 * D, T], F32)
    oTd = dscr("oTd", [D, T], BF16)
    oTs = dscr("oTs", [D, T], BF16)

    S = Sched(nc)

    uid = [0]

    def sb(name, shape, dt):
        uid[0] += 1
        return nc.sbuf_tensor("%s_%d" % (name, uid[0]), shape, dt)

    def pst(name, shape, dt):
        uid[0] += 1
        return nc.psum_tensor("%s_%d" % (name, uid[0]), shape, dt)

    ident = nc.alloc_sbuf_tensor("ident", [128, 128], BF16).ap()
    onesb = nc.alloc_sbuf_tensor("onesb", [128, 128], BF16).ap()
    epsb = nc.alloc_sbuf_tensor("epsb", [128, 1], F32).ap()
    gb = nc.alloc_sbuf_tensor("gb", [128, 4, D], F32).ap()
    bgT = nc.alloc_sbuf_tensor("bgT", [128, 16], F32).ap()
    lt = nc.alloc_sbuf_tensor("lt", [128, 256], F32).ap()
    ltmp = nc.alloc_sbuf_tensor("ltmp", [128, 64], F32).ap()
    s12 = nc.alloc_sbuf_tensor("s12", [128, 2], F32).ap()
    e12 = nc.alloc_sbuf_tensor("e12", [128, 2], F32).ap()
    nl0 = nc.alloc_sbuf_tensor("nl0", [128, 1], F32).ap()
    neglam = nc.alloc_sbuf_tensor("neglam", [128, 1], F32).ap()
    gsub0 = nc.alloc_sbuf_tensor("gsub0", [128, 128], F32).ap()
    gsub = nc.alloc_sbuf_tensor("gsub", [128, 128], F32).ap()
    sk0 = nc.alloc_sbuf_tensor("sk0", [128, 16], F32).ap()
    es = nc.alloc_sbuf_tensor("es", [128, 16], F32).ap()
    Tsw = nc.alloc_sbuf_tensor("Tsw", [128, 16, 384], BF16).ap()

    def setup():
        S.add("pool", lambda e: e.memset(onesb, 1.0), w=["onesb"])
        S.add("pool", lambda e: e.affine_select(out=ident, in_=onesb, pattern=[[1, 128]], compare_op=ALU.is_equal,
                                               fill=0.0, base=0, channel_multiplier=-1), r=["onesb"], w=["ident"])
        S.add("pool", lambda e: e.memset(epsb, 1e-6), w=["epsb"])
        for i, nm in enumerate(["norm_ffn1", "norm_mix", "norm_ffn2", "norm_final"]):
            S.add("sp", lambda e, i=i, nm=nm: e.dma_start(out=gb[:, i, :], in_=W[nm].partition_broadcast(128)),
                  w=[("gb", i)], dma="setup")
        S.add("sp", lambda e: e.dma_start(out=bgT, in_=W["b_gate"].rearrange("(c p) -> p c", p=128),
                                         allow_slow_non_contiguous=True), w=["bgT"], dma="setup")
        S.add("sp", lambda e: e.dma_start(out=lt, in_=W["da_lambda"].partition_broadcast(128)), w=["lt"], dma="setup")
        S.add("sp", lambda e: e.dma_start(out=gsub0, in_=W["da_subnorm"].partition_broadcast(128)), w=["gsub0"], dma="setup")
        S.add("sp", lambda e: e.dma_start(out=sk0, in_=W["swa_sink"].partition_broadcast(128)), w=["sk0"], dma="setup")
        S.add("dve", lambda e: e.tensor_tensor(out=ltmp, in0=lt[:, 0:64], in1=lt[:, 64:128], op=ALU.mult), r=["lt"], w=["ltmp"])
        S.add("dve", lambda e: e.reduce_sum(out=s12[:, 0:1], in_=ltmp, axis=AX.X), r=["ltmp"], w=["s12a"])
        S.add("dve", lambda e: e.tensor_tensor(out=ltmp, in0=lt[:, 128:192], in1=lt[:, 192:256], op=ALU.mult), r=["lt", "s12a"], w=["ltmp"])
        S.add("dve", lambda e: e.reduce_sum(out=s12[:, 1:2], in_=ltmp, axis=AX.X), r=["ltmp"], w=["s12b"])
        S.add("act", lambda e: e.activation(out=e12, in_=s12, func=AF.Exp), r=["s12a", "s12b"], w=["e12"])
        S.add("dve", lambda e: e.tensor_tensor(out=nl0, in0=e12[:, 1:2], in1=e12[:, 0:1], op=ALU.subtract), r=["e12"], w=["nl0"])
        S.add("dve", lambda e: e.tensor_scalar(out=neglam, in0=nl0, scalar1=-0.2, scalar2=None, op0=ALU.add), r=["nl0"], w=["neglam"])
        S.add("dve", lambda e: e.tensor_scalar(out=gsub, in0=gsub0, scalar1=0.8, scalar2=None, op0=ALU.mult), r=["gsub0"], w=["gsub"])
        S.add("act", lambda e: e.activation(out=es, in_=sk0, func=AF.Exp), r=["sk0"], w=["es"])
        with ExitStack() as _st1:
            dswi_t = _st1.enter_context(sb("dswi", [128, 384], F32))
            dswa_t = _st1.enter_context(sb("dswa", [128, 384], F32))
            dswb_t = _st1.enter_context(sb("dswb", [128, 384], F32))
            dswi, dswa, dswb = dswi_t.ap(), dswa_t.ap(), dswb_t.ap()
            S.add("pool", lambda e: e.iota(dswi, [[1, 384]], base=-128, channel_multiplier=-1,
                                           allow_small_or_imprecise_dtypes=True), w=["dswi"])
            S.add("act", lambda e: e.activation(out=dswa, in_=dswi, func=AF.Abs), r=["dswi"], w=["dswa"])
            S.add("pool", lambda e: e.affine_select(out=dswb, in_=dswa, pattern=[[1, 384]], compare_op=ALU.is_ge,
                                                   fill=1.0e6, base=0, channel_multiplier=-1), r=["dswa"], w=["dswb"])
            S.add("pool", lambda e: e.affine_select(out=dswi, in_=dswb, pattern=[[-1, 384]], compare_op=ALU.is_ge,
                                                   fill=1.0e6, base=256, channel_multiplier=1), r=["dswb"], w=["dswi2"])
            for h in range(16):
                sl = 2.0 ** (-8.0 * (h + 1) / 16.0)
                S.add("act", lambda e, h=h, sl=sl: e.activation(out=Tsw[:, h, :], in_=dswi, func=AF.Exp, scale=-sl),
                      r=["dswi2"], w=[("Tsw", h)])
            S.barrier()

    def norm_to_hT(xres, gi, hT, hb, ptr, junk, ssq, lnv, rstd, store=None):
        def stats(t):
            S.add("act", lambda e: e.activation(out=junk, in_=xres[:, t, :], func=AF.Square, scale=1.0 / 32.0,
                                                accum_out=ssq[:, t:t + 1]),
                  r=[("x", t)], w=["junk", ("ssq", t)])
            S.add("act", lambda e: e.activation(out=lnv[:, t:t + 1], in_=ssq[:, t:t + 1], func=AF.Ln, bias=epsb, scale=1.0),
                  r=[("ssq", t)], w=[("lnv", t)])
            S.add("act", lambda e: e.activation(out=rstd[:, t:t + 1], in_=lnv[:, t:t + 1], func=AF.Exp, scale=-0.5),
                  r=[("lnv", t)], w=[("rstd", t)])
        stats(0)
        stats(1)
        for t in range(NT):
            if t + 2 < NT:
                stats(t + 2)
            hbt = hb[t % 2]
            pt = ptr[t % 2]
            S.add("dve", lambda e, t=t, hbt=hbt: e.scalar_tensor_tensor(out=hbt, in0=xres[:, t, :], scalar=rstd[:, t:t + 1],
                                                                     in1=gb[:, gi, :], op0=ALU.mult, op1=ALU.mult),
                  r=[("x", t), ("rstd", t), ("gb", gi)], w=[("hb", t % 2)])
            for kc in range(8):
                S.add("pe", lambda e, kc=kc, hbt=hbt, pt=pt: e.transpose(out=pt[:, kc * 128:(kc + 1) * 128],
                                                                          in_=hbt[:, kc * 128:(kc + 1) * 128], identity=ident),
                      r=[("hb", t % 2), "ident"], w=[("ptr", t % 2)])
            S.add("act", lambda e, t=t, pt=pt: e.activation(out=hT[:, :, t * 128:(t + 1) * 128],
                                                           in_=pt.rearrange("p (k t) -> p k t", k=8), func=AF.Copy),
                  r=[("ptr", t % 2)], w=[("hT", t)])
            if store is not None:
                store(t)

    def ffn(xres, hT, wg_d, wu_d, wd_d, pg, pu, py):
        with ExitStack() as _st2:
            aT_t = _st2.enter_context(sb("aT", [128, NFF, TB], BF16))
            wd_t = _st2.enter_context(sb("wdb", [128, NFF, D], BF16))
            wg0 = _st2.enter_context(sb("wg0", [128, 8, 256], BF16))
            wg1 = _st2.enter_context(sb("wg1", [128, 8, 256], BF16))
            wu0 = _st2.enter_context(sb("wu0", [128, 8, 256], BF16))
            wu1 = _st2.enter_context(sb("wu1", [128, 8, 256], BF16))
            sg0 = _st2.enter_context(sb("sg0", [128, 512], F32))
            sg1 = _st2.enter_context(sb("sg1", [128, 512], F32))
            aT, wdb = aT_t.ap(), wd_t.ap()
            wg = [wg0.ap(), wg1.ap()]
            wu = [wu0.ap(), wu1.ap()]
            sg = [sg0.ap(), sg1.ap()]
            NG = 11

            def load_gu(g):
                s = g % 2
                S.add("pool", lambda e: e.dma_start(out=wg[s], in_=wg_d[:, g * 256:(g + 1) * 256].rearrange("(kc p) f -> p kc f", p=128)),
                      w=[("wg", s)], dma="wg%d" % s)
                S.add("pool", lambda e: e.dma_start(out=wu[s], in_=wu_d[:, g * 256:(g + 1) * 256].rearrange("(kc p) f -> p kc f", p=128)),
                      w=[("wu", s)], dma="wu%d" % s)

            def load_wd(i):
                S.add("pool", lambda e: e.dma_start(out=wdb[:, 2 * i:2 * i + 2, :],
                                                   in_=wd_d[i * 256:(i + 1) * 256, :].rearrange("(c p) f -> p c f", p=128)),
                      w=[("wd", i)], dma="wd")

            load_gu(0)
            load_gu(1)
            cnt = 0
            for g in range(NG):
                s = g % 2
                for c2 in range(2):
                    ffc = g * 2 + c2
                    for sub in range(TB // 512):
                        par = cnt % 2
                        cnt += 1
                        tk = [("hT", t) for t in range(sub * 4, sub * 4 + 4)]
                        for kc in range(8):
                            S.add("pe", lambda e, kc=kc, s=s, c2=c2, sub=sub, par=par: e.matmul(
                                pg[par], lhsT=wg[s][:, kc, c2 * 128:(c2 + 1) * 128], rhs=hT[:, kc, sub * 512:(sub + 1) * 512],
                                start=(kc == 0), stop=(kc == 7)), r=[("wg", s)] + tk, w=[("pg", par)])
                        for kc in range(8):
                            S.add("pe", lambda e, kc=kc, s=s, c2=c2, sub=sub, par=par: e.matmul(
                                pu[par], lhsT=wu[s][:, kc, c2 * 128:(c2 + 1) * 128], rhs=hT[:, kc, sub * 512:(sub + 1) * 512],
                                start=(kc == 0), stop=(kc == 7)), r=[("wu", s)] + tk, w=[("pu", par)])
                        S.add("act", lambda e, par=par: e.activation(out=sg[par], in_=pg[par], func=AF.Silu),
                              r=[("pg", par)], w=[("sg", par)])
                        S.add("dve", lambda e, par=par, ffc=ffc, sub=sub: e.tensor_tensor(
                            out=aT[:, ffc, sub * 512:(sub + 1) * 512], in0=sg[par], in1=pu[par], op=ALU.mult),
                            r=[("sg", par), ("pu", par)], w=[("aT", ffc, sub)])
                if g + 2 < NG:
                    load_gu(g + 2)
                load_wd(g)
            cnt = 0
            for t in range(NT):
                for half in range(2):
                    par = cnt % 2
                    cnt += 1
                    for ffc in range(NFF):
                        S.add("pe", lambda e, t=t, half=half, ffc=ffc, par=par: e.matmul(
                            py[par], lhsT=aT[:, ffc, t * 128:(t + 1) * 128], rhs=wdb[:, ffc, half * 512:(half + 1) * 512],
                            start=(ffc == 0), stop=(ffc == NFF - 1)),
                            r=[("aT", ffc, t // 4), ("wd", ffc // 2)], w=[("py", par)])
                    S.add("dve", lambda e, t=t, half=half, par=par: e.scalar_tensor_tensor(
                        out=xres[:, t, half * 512:(half + 1) * 512], in0=py[par], scalar=0.5,
                        in1=xres[:, t, half * 512:(half + 1) * 512], op0=ALU.mult, op1=ALU.add),
                        r=[("py", par), ("x", t)], w=[("x", t)])

    def phase_A(blk):
        r0 = blk * TB
        with ExitStack() as _st3:
            xres_t = _st3.enter_context(sb("xres", [128, NT, D], F32))
            hT_t = _st3.enter_context(sb("hT", [128, 8, TB], BF16))
            hb0 = _st3.enter_context(sb("hb0", [128, D], BF16))
            hb1 = _st3.enter_context(sb("hb1", [128, D], BF16))
            junk_t = _st3.enter_context(sb("junk", [128, D], BF16))
            ssq_t = _st3.enter_context(sb("ssq", [128, NT], F32))
            lnv_t = _st3.enter_context(sb("lnv", [128, NT], F32))
            rstd_t = _st3.enter_context(sb("rstd", [128, NT], F32))
            xres, hT = xres_t.ap(), hT_t.ap()
            hb = [hb0.ap(), hb1.ap()]
            junk, ssq, lnv, rstd = junk_t.ap(), ssq_t.ap(), lnv_t.ap(), rstd_t.ap()
            with ExitStack() as _st4:
                p0 = _st4.enter_context(pst("ptr0", [128, 1024], BF16))
                p1 = _st4.enter_context(pst("ptr1", [128, 1024], BF16))
                pg0 = _st4.enter_context(pst("pg0", [128, 512], F32))
                pg1 = _st4.enter_context(pst("pg1", [128, 512], F32))
                pu0 = _st4.enter_context(pst("pu0", [128, 512], F32))
                pu1 = _st4.enter_context(pst("pu1", [128, 512], F32))
                py0 = _st4.enter_context(pst("py0", [128, 512], F32))
                py1 = _st4.enter_context(pst("py1", [128, 512], F32))
                ptr = [p0.ap(), p1.ap()]
                pg = [pg0.ap(), pg1.ap()]
                pu = [pu0.ap(), pu1.ap()]
                py = [py0.ap(), py1.ap()]
                for t in range(NT):
                    S.add("sp", lambda e, t=t: e.dma_start(out=xres[:, t, :], in_=x_in[r0 + t * 128:r0 + (t + 1) * 128, :]),
                          w=[("x", t)], dma="x%d" % t)
                norm_to_hT(xres, 0, hT, hb, ptr, junk, ssq, lnv, rstd)
                ffn(xres, hT, W["ffn1_gate"], W["ffn1_up"], W["ffn1_down"], pg, pu, py)

                def store_x1(t):
                    S.add("sp", lambda e, t=t: e.dma_start(out=x1s[r0 + t * 128:r0 + (t + 1) * 128, :], in_=xres[:, t, :]),
                          r=[("x", t)], dma="st")
                norm_to_hT(xres, 1, hT, hb, ptr, junk, ssq, lnv, rstd, store=store_x1)
                S.barrier()
            with ExitStack() as _st5:
                wi0 = _st5.enter_context(sb("wi0", [128, 8, 256], BF16))
                wi1 = _st5.enter_context(sb("wi1", [128, 8, 256], BF16))
                wi2 = _st5.enter_context(sb("wi2", [128, 8, 256], BF16))
                sgb0 = _st5.enter_context(sb("sgb0", [128, TB], BF16))
                sgb1 = _st5.enter_context(sb("sgb1", [128, TB], BF16))
                sgf0 = _st5.enter_context(sb("sgf0", [128, TB], F32))
                sgf1 = _st5.enter_context(sb("sgf1", [128, TB], F32))
                svb0 = _st5.enter_context(sb("svb0", [128, NT, 256], BF16))
                svb1 = _st5.enter_context(sb("svb1", [128, NT, 256], BF16))
                pq0 = _st5.enter_context(pst("pq0", [128, 512], F32))
                pq1 = _st5.enter_context(pst("pq1", [128, 512], F32))
                pq2 = _st5.enter_context(pst("pq2", [128, 512], F32))
                pq3 = _st5.enter_context(pst("pq3", [128, 512], F32))
                wi = [wi0.ap(), wi1.ap(), wi2.ap()]
                sgb = [sgb0.ap(), sgb1.ap()]
                sgf = [sgf0.ap(), sgf1.ap()]
                svb = [svb0.ap(), svb1.ap()]
                pq = [pq0.ap(), pq1.ap(), pq2.ap(), pq3.ap()]
                NG = 26
                allh = [("hT", t) for t in range(NT)]

                def load_wi(g):
                    s = g % 3
                    S.add("pool", lambda e: e.dma_start(out=wi[s], in_=W["w_in"][:, g * 256:(g + 1) * 256].rearrange("(kc p) f -> p kc f", p=128)),
                          w=[("wi", s)], dma="wi%d" % s)
                load_wi(0)
                load_wi(1)
                load_wi(2)
                pcnt = 0
                ccnt = 0
                vcnt = 0
                for g in range(NG):
                    s = g % 3
                    col0 = g * 256
                    if (8 <= g < 12) or g == 17:
                        sv = svb[vcnt % 2]
                        svk = ("svb", vcnt % 2)
                        vcnt += 1
                        for t in range(NT):
                            pp = pq[pcnt % 4]
                            ppk = ("pq", pcnt % 4)
                            pcnt += 1
                            for kc in range(8):
                                S.add("pe", lambda e, kc=kc, t=t, s=s, pp=pp: e.matmul(
                                    pp[:, 0:256], lhsT=hT[:, kc, t * 128:(t + 1) * 128], rhs=wi[s][:, kc, :],
                                    start=(kc == 0), stop=(kc == 7)), r=[("wi", s), ("hT", t)], w=[ppk])
                            eng = "dve" if t % 2 == 0 else "act"
                            if eng == "dve":
                                S.add("dve", lambda e, t=t, pp=pp, sv=sv: e.tensor_copy(out=sv[:, t, :], in_=pp[:, 0:256]),
                                      r=[ppk], w=[(svk, t)])
                            else:
                                S.add("act", lambda e, t=t, pp=pp, sv=sv: e.activation(out=sv[:, t, :], in_=pp[:, 0:256], func=AF.Copy),
                                      r=[ppk], w=[(svk, t)])
                        if g == 17:
                            dst = vs[r0:r0 + TB, :].rearrange("(t p) f -> p t f", p=128)
                        else:
                            dst = vd[r0:r0 + TB, (g - 8) * 256:(g - 7) * 256].rearrange("(t p) f -> p t f", p=128)
                        S.add("sp", lambda e, dst=dst, sv=sv: e.dma_start(out=dst, in_=sv),
                              r=[(svk, t) for t in range(NT)], dma="stv%d" % svk[1])
                    else:
                        for c2 in range(2):
                            col = col0 + c2 * 128
                            isgate = col >= 4608
                            stg = (sgf if isgate else sgb)[ccnt % 2]
                            stk = ("sgf" if isgate else "sgb", ccnt % 2)
                            ccnt += 1
                            for sub in range(TB // 512):
                                pp = pq[pcnt % 4]
                                ppk = ("pq", pcnt % 4)
                                pcnt += 1
                                tk = [("hT", t) for t in range(sub * 4, sub * 4 + 4)]
                                for kc in range(8):
                                    S.add("pe", lambda e, kc=kc, s=s, c2=c2, sub=sub, pp=pp: e.matmul(
                                        pp, lhsT=wi[s][:, kc, c2 * 128:(c2 + 1) * 128], rhs=hT[:, kc, sub * 512:(sub + 1) * 512],
                                        start=(kc == 0), stop=(kc == 7)), r=[("wi", s)] + tk, w=[ppk])
                                if isgate:
                                    gc = (col - 4608) // 128
                                    S.add("act", lambda e, pp=pp, stg=stg, sub=sub, gc=gc: e.activation(
                                        out=stg[:, sub * 512:(sub + 1) * 512], in_=pp, func=AF.Sigmoid, bias=bgT[:, gc:gc + 1], scale=1.0),
                                        r=[ppk, "bgT"], w=[(stk, sub)])
                                elif sub % 2 == 0:
                                    S.add("dve", lambda e, pp=pp, stg=stg, sub=sub: e.tensor_copy(out=stg[:, sub * 512:(sub + 1) * 512], in_=pp),
                                          r=[ppk], w=[(stk, sub)])
                                else:
                                    S.add("act", lambda e, pp=pp, stg=stg, sub=sub: e.activation(out=stg[:, sub * 512:(sub + 1) * 512], in_=pp, func=AF.Copy),
                                          r=[ppk], w=[(stk, sub)])
                            if col < 1024:
                                dst = qTd[col:col + 128, r0:r0 + TB]
                            elif col < 2048:
                                dst = kTd[col - 1024:col - 1024 + 128, r0:r0 + TB]
                            elif col < 4096:
                                dst = qTs[col - 3072:col - 3072 + 128, r0:r0 + TB]
                            elif col < 4352:
                                dst = kTs[col - 4096:col - 4096 + 128, r0:r0 + TB]
                            else:
                                dst = gTs[col - 4608:col - 4608 + 128, r0:r0 + TB]
                            S.add("sp", lambda e, dst=dst, stg=stg: e.dma_start(out=dst, in_=stg),
                                  r=[(stk, sub) for sub in range(TB // 512)], dma="st%s%d" % (stk[0], stk[1]))
                    if g + 3 < NG:
                        load_wi(g + 3)
                S.barrier()

    def store_oT(otok, dstT, c0, tagp, barrier=True):
        with ExitStack() as _st6:
            ost0 = _st6.enter_context(sb("ost0", [128, 8, 512], BF16))
            ost1 = _st6.enter_context(sb("ost1", [128, 8, 512], BF16))
            pot0 = _st6.enter_context(pst("pot0", [128, 1024], BF16))
            pot1 = _st6.enter_context(pst("pot1", [128, 1024], BF16))
            ost = [ost0.ap(), ost1.ap()]
            pot = [pot0.ap(), pot1.ap()]
            for qb in range(4):
                st = ost[qb % 2]
                for qi in range(4):
                    n = qb * 4 + qi
                    pp = pot[n % 2]
                    for kc in range(8):
                        S.add("pe", lambda e, n=n, kc=kc, pp=pp: e.transpose(out=pp[:, kc * 128:(kc + 1) * 128],
                                                                           in_=otok[:, n, kc * 128:(kc + 1) * 128], identity=ident),
                              r=[("otok", n), "ident"], w=[("pot", n % 2)])
                    if n % 2 == 0:
                        S.add("act", lambda e, pp=pp, st=st, qi=qi: e.activation(out=st[:, :, qi * 128:(qi + 1) * 128],
                                                                                in_=pp.rearrange("p (k t) -> p k t", k=8), func=AF.Copy),
                              r=[("pot", n % 2)], w=[("ost", qb % 2, qi)])
                    else:
                        S.add("dve", lambda e, pp=pp, st=st, qi=qi: e.tensor_copy(out=st[:, :, qi * 128:(qi + 1) * 128],
                                                                                 in_=pp.rearrange("p (k t) -> p k t", k=8)),
                              r=[("pot", n % 2)], w=[("ost", qb % 2, qi)])
                S.add("sp", lambda e, st=st, qb=qb: e.dma_start(
                    out=dstT[:, c0 + qb * 512:c0 + (qb + 1) * 512].rearrange("(k p) t -> p k t", p=128), in_=st),
                    r=[("ost", qb % 2, qi) for qi in range(4)], dma="sto%d" % (qb % 2))
            if barrier:
                S.barrier()

    def phase_B1(seq, otok):
        c0 = seq * SEQ
        NM = 3968
        if True:
            with ExitStack() as _st8:
                qT_t = _st8.enter_context(sb("qT", [128, 2, SEQ], BF16))
                kT_t = _st8.enter_context(sb("kT", [128, 2, SEQ], BF16))
                td0 = _st8.enter_context(sb("td0", [128, NM], BF16))
                td1 = _st8.enter_context(sb("td1", [128, NM], BF16))
                erl = [_st8.enter_context(sb("er%d" % i_, [128, 768], BF16)) for i_ in range(3)]
                va_t = _st8.enter_context(sb("vaug", [128, 16, 8, 129], BF16))
                dd_t = _st8.enter_context(sb("dd", [128, NM], F32))
                mhi_t = _st8.enter_context(sb("mhi", [128, NM], BF16))
                mlo_t = _st8.enter_context(sb("mlo", [128, NM], BF16))
                ih_t = _st8.enter_context(sb("ih", [128, 8, 128], BF16))
                etl = [_st8.enter_context(sb("et%d" % i_, [128, 768], BF16)) for i_ in range(9)]
                rd_t = _st8.enter_context(sb("rd", [128, 2, 4], F32))
                rl2_t = _st8.enter_context(sb("rl2", [128, 4], F32))
                t1_t = _st8.enter_context(sb("t1", [128, 3, 128], F32))
                of_t = _st8.enter_context(sb("of", [128, 3, 128], F32))
                jk2_t = _st8.enter_context(sb("jk2", [128, 128], BF16))
                jk2f_t = _st8.enter_context(sb("jk2f", [128, 128], F32))
                ss2_t = _st8.enter_context(sb("ss2", [128, 4], F32))
                ln2_t = _st8.enter_context(sb("ln2", [128, 4], F32))
                rs2_t = _st8.enter_context(sb("rs2", [128, 4], F32))
                pcl = [_st8.enter_context(sb("pc%d" % i_, [128, 387], F32)) for i_ in range(2)]
                psl = [_st8.enter_context(pst("ps%d" % i_, [128, 1024], F32)) for i_ in range(3)]
                pol = [_st8.enter_context(pst("po%d" % i_, [128, 512], F32)) for i_ in range(2)]
                qT, kT, vaug = qT_t.ap(), kT_t.ap(), va_t.ap()
                dd = dd_t.ap()
                Mhi, Mlo, Ih = mhi_t.ap(), mlo_t.ap(), ih_t.ap()
                td = [td0.ap(), td1.ap()]
                er = [t_.ap() for t_ in erl]
                et = [t_.ap() for t_ in etl]
                rd, rl2, t1, of = rd_t.ap(), rl2_t.ap(), t1_t.ap(), of_t.ap()
                jk2, ss2, ln2, rs2 = jk2_t.ap(), ss2_t.ap(), ln2_t.ap(), rs2_t.ap()
                jk2f = jk2f_t.ap()
                ps = [t_.ap() for t_ in psl]
                po = [t_.ap() for t_ in pol]
                pc = [t_.ap() for t_ in pcl]
                def ld_qk(h):
                    sl_ = h % 2
                    S.add("sp", lambda e: e.dma_start(out=qT[:, sl_, :], in_=qTd[h * 128:(h + 1) * 128, c0:c0 + SEQ]),
                          w=[("qk", sl_)], dma="bqk%d" % sl_)
                    S.add("sp", lambda e: e.dma_start(out=kT[:, sl_, :], in_=kTd[h * 128:(h + 1) * 128, c0:c0 + SEQ]),
                          w=[("qk", sl_)], dma="bqk%d" % sl_)
                ld_qk(0)
                S.add("pool", lambda e: e.memset(vaug[:, :, :, 128:129], 1.0), w=["vones"])
                for kc in range(16):
                    S.add("sp", lambda e, kc=kc: e.dma_start(
                        out=vaug[:, kc, :, 0:128],
                        in_=vd[c0 + kc * 128:c0 + (kc + 1) * 128, :].rearrange("p (h e) -> p h e", h=8)),
                        w=[("v", kc)], dma="bv")
                ld_qk(1)
                S.add("pool", lambda e: e.iota(dd, [[1, NM]], base=-1920, channel_multiplier=-1,
                                               allow_small_or_imprecise_dtypes=True), w=["dd"])
                S.add("act", lambda e: e.activation(out=dd, in_=dd, func=AF.Abs), r=["dd"], w=["dd"])
                S.add("act", lambda e: e.activation(out=Mhi, in_=dd, func=AF.Copy, scale=-1.0), r=["dd"], w=["Mhi"])
                S.add("dve", lambda e: e.scalar_tensor_tensor(out=Mlo, in0=dd, scalar=-1.0, in1=Mhi, op0=ALU.mult, op1=ALU.subtract),
                      r=["dd", "Mhi"], w=["Mlo"])
                for h in range(8):
                    S.add("dve", lambda e, h=h: e.tensor_scalar(out=Ih[:, h, :], in0=ident, scalar1=2.0 ** (2 - h), scalar2=None, op0=ALU.mult),
                          r=["ident"], w=[("Ih", h)])
                qblocks = [(0, 3), (3, 3), (6, 3), (9, 3), (12, 2), (14, 2)]
                steps = [(h, bi_, kc) for h in range(8) for bi_ in range(len(qblocks)) for kc in range(16)]
                NS = len(steps)
                LAG = 7
                NPS = 3
                NB = 3
                NBE = 9

                TDP = NM // 8

                def gen_td(h, piece):
                    sl = 2.0 ** (-(h + 1))
                    tdh = td[h % 2]
                    S.add("act", lambda e: e.activation(out=tdh[:, piece * TDP:(piece + 1) * TDP],
                                                        in_=dd[:, piece * TDP:(piece + 1) * TDP], func=AF.Exp, scale=-sl),
                          r=["dd"], w=[("td", h % 2, piece)])

                def b1_qk(i):
                    h, bi_, kc = steps[i]
                    t0_, nq = qblocks[bi_]
                    q0, k0, Wd = t0_ * 128, kc * 128, nq * 128
                    slot = i % NPS
                    pt_ = ps[slot]
                    nbuf = i % NBE
                    nbr = i % NB
                    hs = h % 2
                    m0 = q0 - k0 + 1920
                    tdh = td[h % 2]
                    use_lo = h >= 4
                    on_pe = (i % 5 == 2) if use_lo else (i % 5 in (1, 3))
                    if kc == 0 and h + 1 < 8:
                        gen_td(h + 1, bi_)
                        if bi_ == 5:
                            gen_td(h + 1, 6)
                            gen_td(h + 1, 7)
                        if bi_ == 0 and h >= 1:
                            ld_qk(h + 1)
                    S.add("pe", lambda e: e.matmul(
                        pt_[:, 0:Wd], lhsT=kT[0:64, hs, k0:k0 + 128], rhs=qT[0:64, hs, q0:q0 + Wd],
                        start=True, stop=(not on_pe)), r=[("qk", hs)], w=[("ps", slot, 0)])
                    S.add("pe", lambda e: e.matmul(
                        pt_[:, 512:512 + Wd], lhsT=kT[64:128, hs, k0:k0 + 128], rhs=qT[64:128, hs, q0:q0 + Wd],
                        start=True, stop=(not on_pe)), r=[("qk", hs)], w=[("ps", slot, 1)])
                    if on_pe:
                        for comp in range(2):
                            S.add("pe", lambda e, comp=comp: e.matmul(
                                pt_[:, comp * 512:comp * 512 + Wd], lhsT=Ih[:, h, :], rhs=Mhi[:, m0:m0 + Wd],
                                start=False, stop=(not use_lo)), r=[("Ih", h), "Mhi"], w=[("ps", slot, comp)])
                        if use_lo:
                            for comp in range(2):
                                S.add("pe", lambda e, comp=comp: e.matmul(
                                    pt_[:, comp * 512:comp * 512 + Wd], lhsT=Ih[:, h, :], rhs=Mlo[:, m0:m0 + Wd],
                                    start=False, stop=True), r=[("Ih", h), "Mlo"], w=[("ps", slot, comp)])
                        for comp in range(2):
                            S.add("act", lambda e, comp=comp: e.activation(
                                out=et[nbuf][:, comp * 384:comp * 384 + Wd],
                                in_=pt_[:, comp * 512:comp * 512 + Wd], func=AF.Exp, scale=0.125),
                                r=[("ps", slot, comp)], w=[("et", nbuf, comp)])
                    else:
                        for comp in range(2):
                            S.add("act", lambda e, comp=comp: e.activation(
                                out=er[nbr][:, comp * 384:comp * 384 + Wd],
                                in_=pt_[:, comp * 512:comp * 512 + Wd], func=AF.Exp, scale=0.125),
                                r=[("ps", slot, comp)], w=[("er", nbr, comp)])
                            S.add("dve", lambda e, comp=comp: e.tensor_tensor(
                                out=et[nbuf][:, comp * 384:comp * 384 + Wd], in0=er[nbr][:, comp * 384:comp * 384 + Wd],
                                in1=tdh[:, m0:m0 + Wd], op=ALU.mult),
                                r=[("er", nbr, comp)] + [("td", h % 2, p_) for p_ in range(8)], w=[("et", nbuf, comp)])

                def b1_av(i):
                    h, bi_, kc = steps[i]
                    t0_, nq = qblocks[bi_]
                    nbuf = i % NBE
                    for comp in range(2):
                        for qi in range(nq):
                            S.add("pe", lambda e, qi=qi, comp=comp: e.matmul(
                                po[comp][:, qi * 129:(qi + 1) * 129],
                                lhsT=et[nbuf][:, comp * 384 + qi * 128:comp * 384 + (qi + 1) * 128],
                                rhs=vaug[:, kc, h, :], start=(kc == 0 and qi == 0), stop=(kc == 15 and qi == nq - 1)),
                                r=[("et", nbuf, comp), ("v", kc), "vones"], w=[("po", comp)])
                    if kc != 15:
                        return
                    S.add("dve", lambda e: e.tensor_copy(out=pc[0][:, 0:nq * 129], in_=po[0][:, 0:nq * 129]), r=[("po", 0)], w=[("pc", 0)])
                    S.add("dve", lambda e: e.tensor_copy(out=pc[1][:, 0:nq * 129], in_=po[1][:, 0:nq * 129]), r=[("po", 1)], w=[("pc", 1)])
                    pv0 = pc[0].rearrange("p (q e) -> p q e", e=129)
                    pv1 = pc[1].rearrange("p (q e) -> p q e", e=129)
                    S.add("dve", lambda e: e.reciprocal(out=rd[:, 0, 0:nq], in_=pv0[:, 0:nq, 128]), r=[("pc", 0)], w=["rd0"])
                    S.add("dve", lambda e: e.reciprocal(out=rd[:, 1, 0:nq], in_=pv1[:, 0:nq, 128]), r=[("pc", 1)], w=["rd1"])
                    S.add("dve", lambda e: e.tensor_scalar(out=rl2[:, 0:nq], in0=rd[:, 1, 0:nq], scalar1=neglam, scalar2=None, op0=ALU.mult),
                          r=["rd1", "neglam"], w=["rl2"])
                    for qi in range(nq):
                        S.add("dve", lambda e, qi=qi: e.tensor_scalar(out=t1[:, qi, :], in0=pv0[:, qi, 0:128], scalar1=rd[:, 0, qi:qi + 1],
                                                                      scalar2=None, op0=ALU.mult),
                              r=[("pc", 0), "rd0"], w=[("t1", qi)])
                        S.add("dve", lambda e, qi=qi: e.scalar_tensor_tensor(out=of[:, qi, :], in0=pv1[:, qi, 0:128], scalar=rl2[:, qi:qi + 1],
                                                                             in1=t1[:, qi, :], op0=ALU.mult, op1=ALU.add),
                              r=[("pc", 1), "rl2", ("t1", qi)], w=[("of", qi)])

                    def part_b(nq=nq):
                        for qi in range(nq):
                            S.add("act", lambda e, qi=qi: e.activation(out=jk2, in_=of[:, qi, :], func=AF.Square, scale=128.0 ** -0.5,
                                                                       accum_out=ss2[:, qi:qi + 1]),
                                  r=[("of", qi)], w=["jk2", ("ss2", qi)])
                        S.add("act", lambda e: e.activation(out=ln2[:, 0:nq], in_=ss2[:, 0:nq], func=AF.Ln, bias=epsb, scale=1.0),
                              r=[("ss2", qi) for qi in range(nq)], w=["ln2"])
                        S.add("act", lambda e: e.activation(out=rs2[:, 0:nq], in_=ln2[:, 0:nq], func=AF.Exp, scale=-0.5),
                              r=["ln2"], w=["rs2"])

                    def part_c(nq=nq, t0_=t0_, h=h):
                        for qi in range(nq):
                            n = t0_ + qi
                            S.add("dve", lambda e, qi=qi, n=n: e.scalar_tensor_tensor(
                                out=otok[:, n, h * 128:(h + 1) * 128], in0=of[:, qi, :], scalar=rs2[:, qi:qi + 1], in1=gsub,
                                op0=ALU.mult, op1=ALU.mult), r=[("of", qi), "rs2", "gsub"], w=[("otok", n, h)])
                    deferred.setdefault(i + 2, []).append(part_b)
                    deferred.setdefault(i + 5, []).append(part_c)

                deferred = {}
                for p_ in range(8):
                    gen_td(0, p_)
                for j in range(NS + LAG):
                    if j < NS:
                        b1_qk(j)
                    i = j - LAG
                    if i >= 0:
                        b1_av(i)
                        for f_ in deferred.pop(i, []):
                            f_()
                for k_ in sorted(deferred):
                    for f_ in deferred[k_]:
                        f_()
                S.barrier()
            pass

    def phase_B2(seq, pre=None):
        c0 = seq * SEQ
        with ExitStack() as _st9:
            otok_t = _st9.enter_context(sb("otok2", [128, 16, D], BF16))
            otok = otok_t.ap()
            with ExitStack() as _st10:
                q_t = _st10.enter_context(sb("qsw", [128, 8, SEQ], BF16))
                k_t = _st10.enter_context(sb("ksw", [128, 4, SEQ], BF16))
                v_t = _st10.enter_context(sb("vsw", [128, 16, 4, 65], BF16))
                er0 = _st10.enter_context(sb("er0", [128, 3, 512], BF16))
                er1 = _st10.enter_context(sb("er1", [128, 3, 512], BF16))
                et0 = _st10.enter_context(sb("et0", [128, 3, 512], BF16))
                et1 = _st10.enter_context(sb("et1", [128, 3, 512], BF16))
                dn0 = _st10.enter_context(sb("dn0", [128, 4], F32))
                dn1 = _st10.enter_context(sb("dn1", [128, 4], F32))
                rn0 = _st10.enter_context(sb("rn0", [128, 4], F32))
                rn1 = _st10.enter_context(sb("rn1", [128, 4], F32))
                pssl = [[_st10.enter_context(pst("pss%d%d" % (a_, b_), [128, 512], F32)) for b_ in range(2)] for a_ in range(2)]
                pos0 = _st10.enter_context(pst("pos0", [128, 512], F32))
                pos1 = _st10.enter_context(pst("pos1", [128, 512], F32))
                qsw, ksw, vsw = q_t.ap(), k_t.ap(), v_t.ap()
                er = [er0.ap(), er1.ap()]
                et = [et0.ap(), et1.ap()]
                dn = [dn0.ap(), dn1.ap()]
                rn = [rn0.ap(), rn1.ap()]
                pss = [[t_.ap() for t_ in row_] for row_ in pssl]
                pos = [pos0.ap(), pos1.ap()]
                for g in range(4):
                    for hf in range(2):
                        rw = (4 * g + 2 * hf) * 64
                        S.add("sp", lambda e, g=g, hf=hf, rw=rw: e.dma_start(
                            out=qsw[hf * 64:(hf + 1) * 64, 2 * g:2 * g + 2, :],
                            in_=qTs[rw:rw + 128, c0:c0 + SEQ].rearrange("(j d) t -> d j t", d=64)),
                            w=[("qsw", g, hf)], dma="b2l")
                for hf in range(2):
                    S.add("sp", lambda e, hf=hf: e.dma_start(
                        out=ksw[hf * 64:(hf + 1) * 64, :, :],
                        in_=kTs[:, c0:c0 + SEQ].rearrange("(g d) t -> d g t", d=64)), w=[("ksw", hf)], dma="b2l")
                S.add("pool", lambda e: e.memset(vsw[:, :, :, 64:65], 1.0), w=["vones2"])
                for kc in range(16):
                    S.add("sp", lambda e, kc=kc: e.dma_start(
                        out=vsw[:, kc, :, 0:64],
                        in_=vs[c0 + kc * 128:c0 + (kc + 1) * 128, :].rearrange("p (g e) -> p g e", g=4)),
                        w=[("vsw", kc // 4)], dma="b2l")
                if pre is not None:
                    pre()
                steps2 = [(n, g) for n in range(16) for g in range(4)]

                def b2_qk(i):
                    n, g = steps2[i]
                    par = i % 2
                    blocks = [j for j in (n - 1, n, n + 1) if 0 <= j < 16]
                    nb = len(blocks)
                    for hf in range(2):
                        for bk in range(2):
                            bis = [bi for bi in range(nb) if (bi // 2) == bk]
                            if not bis:
                                continue
                            for bi in bis:
                                j = blocks[bi]
                                for jj in range(2):
                                    first = (bi == bis[0] and jj == 0)
                                    last = (bi == bis[-1] and jj == 1)
                                    cc = (bi % 2) * 256 + jj * 128
                                    S.add("pe", lambda e, bk=bk, j=j, hf=hf, jj=jj, cc=cc, first=first, last=last: e.matmul(
                                        pss[hf][bk][:, cc:cc + 128],
                                        lhsT=ksw[hf * 64:(hf + 1) * 64, g, j * 128:(j + 1) * 128],
                                        rhs=qsw[hf * 64:(hf + 1) * 64, 2 * g + jj, n * 128:(n + 1) * 128],
                                        start=first, stop=last),
                                        r=[("qsw", g, hf), ("ksw", hf)], w=[("pss", hf, bk)])
                            for bi in bis:
                                cc = (bi % 2) * 256
                                S.add("act", lambda e, bi=bi, hf=hf, bk=bk, cc=cc: e.activation(
                                    out=er[par][:, bi, hf * 256:(hf + 1) * 256], in_=pss[hf][bk][:, cc:cc + 256],
                                    func=AF.Exp, scale=0.125),
                                    r=[("pss", hf, bk)], w=[("er", par, bi, hf)])
                    for bi, j in enumerate(blocks):
                        ri = 1 - (j - n)
                        S.add("dve", lambda e, bi=bi, ri=ri: e.tensor_tensor(
                            out=et[par][:, bi, :].rearrange("p (h q) -> p h q", h=4),
                            in0=er[par][:, bi, :].rearrange("p (h q) -> p h q", h=4),
                            in1=Tsw[:, 4 * g:4 * g + 4, ri * 128:(ri + 1) * 128], op=ALU.mult),
                            r=[("er", par, bi, 0), ("er", par, bi, 1)], w=[("et", par, bi)])

                def b2_av(i):
                    n, g = steps2[i]
                    par = i % 2
                    blocks = [j for j in (n - 1, n, n + 1) if 0 <= j < 16]
                    nb = len(blocks)
                    for hh in range(4):
                        for bi, j in enumerate(blocks):
                            S.add("pe", lambda e, hh=hh, bi=bi, j=j: e.matmul(
                                pos[par][:, hh * 65:(hh + 1) * 65], lhsT=et[par][:, bi, hh * 128:(hh + 1) * 128],
                                rhs=vsw[:, j, g, :], start=(bi == 0 and hh == 0), stop=(bi == nb - 1 and hh == 3)),
                                r=[("et", par, bi), ("vsw", j // 4), "vones2"], w=[("pos", par)])
                    pv = pos[par][:, 0:260].rearrange("p (h e) -> p h e", h=4)
                    S.add("dve", lambda e: e.tensor_tensor(
                        out=dn[par], in0=pv[:, :, 64], in1=es[:, 4 * g:4 * g + 4], op=ALU.add),
                        r=[("pos", par), "es"], w=[("dn", par)])
                    S.add("dve", lambda e: e.reciprocal(out=rn[par], in_=dn[par]), r=[("dn", par)], w=[("rn", par)])
                    S.add("dve", lambda e: e.tensor_tensor(
                        out=otok[:, n, 4 * g * 64:(4 * g + 4) * 64].rearrange("p (h e) -> p h e", h=4),
                        in0=pv[:, :, 0:64], in1=rn[par].unsqueeze(2).broadcast_to([128, 4, 64]), op=ALU.mult),
                        r=[("pos", par), ("rn", par)], w=[("otok", n, g)])

                b2_qk(0)
                for i in range(len(steps2)):
                    if i + 1 < len(steps2):
                        b2_qk(i + 1)
                    b2_av(i)
                S.barrier()
            store_oT(otok, oTs, c0, "s")

    def phase_C(blk):
        r0 = blk * TB
        with ExitStack() as _st11:
            xres_t = _st11.enter_context(sb("xres", [128, NT, D], F32))
            hT_t = _st11.enter_context(sb("hT", [128, 8, TB], BF16))
            hb0 = _st11.enter_context(sb("hb0", [128, D], BF16))
            hb1 = _st11.enter_context(sb("hb1", [128, D], BF16))
            junk_t = _st11.enter_context(sb("junk", [128, D], BF16))
            ssq_t = _st11.enter_context(sb("ssq", [128, NT], F32))
            lnv_t = _st11.enter_context(sb("lnv", [128, NT], F32))
            rstd_t = _st11.enter_context(sb("rstd", [128, NT], F32))
            xres, hT = xres_t.ap(), hT_t.ap()
            hb = [hb0.ap(), hb1.ap()]
            junk, ssq, lnv, rstd = junk_t.ap(), ssq_t.ap(), lnv_t.ap(), rstd_t.ap()
            with ExitStack() as _st12:
                oTa_t = _st12.enter_context(sb("oTa", [128, 8, TB], BF16))
                oTb_t = _st12.enter_context(sb("oTb", [128, 8, TB], BF16))
                PA_t = _st12.enter_context(sb("PA", [128, 8, D], BF16))
                PB_t = _st12.enter_context(sb("PB", [128, 8, D], BF16))
                WO_t = _st12.enter_context(sb("WO", [128, 8, D], BF16))
                mT_t = _st12.enter_context(sb("mT", [128, 8, TB], BF16))
                ga0 = _st12.enter_context(sb("ga0", [128, TB], F32))
                ga1 = _st12.enter_context(sb("ga1", [128, TB], F32))
                gb0 = _st12.enter_context(sb("gb0", [128, TB], F32))
                gb1 = _st12.enter_context(sb("gb1", [128, TB], F32))
                m10 = _st12.enter_context(sb("m10", [128, 512], F32))
                m11 = _st12.enter_context(sb("m11", [128, 512], F32))
                m20 = _st12.enter_context(sb("m20", [128, 512], F32))
                m21 = _st12.enter_context(sb("m21", [128, 512], F32))
                p0 = _st12.enter_context(pst("ptr0", [128, 1024], BF16))
                p1 = _st12.enter_context(pst("ptr1", [128, 1024], BF16))
                pa0 = _st12.enter_context(pst("pa0", [128, 512], F32))
                pa1 = _st12.enter_context(pst("pa1", [128, 512], F32))
                pb0 = _st12.enter_context(pst("pb0", [128, 512], F32))
                pb1 = _st12.enter_context(pst("pb1", [128, 512], F32))
                py0 = _st12.enter_context(pst("py0", [128, 512], F32))
                py1 = _st12.enter_context(pst("py1", [128, 512], F32))
                oTa, oTb, PA, PB, WO, mT = oTa_t.ap(), oTb_t.ap(), PA_t.ap(), PB_t.ap(), WO_t.ap(), mT_t.ap()
                ga = [ga0.ap(), ga1.ap()]
                gbt = [gb0.ap(), gb1.ap()]
                m1 = [m10.ap(), m11.ap()]
                m2 = [m20.ap(), m21.ap()]
                ptr = [p0.ap(), p1.ap()]
                pa = [pa0.ap(), pa1.ap()]
                pb = [pb0.ap(), pb1.ap()]
                py = [py0.ap(), py1.ap()]
                S.add("sp", lambda e: e.dma_start(out=oTa, in_=oTd[:, r0:r0 + TB].rearrange("(k p) t -> p k t", p=128)), w=["oTa"], dma="co")
                S.add("sp", lambda e: e.dma_start(out=oTb, in_=oTs[:, r0:r0 + TB].rearrange("(k p) t -> p k t", p=128)), w=["oTb"], dma="co")
                for hlf in range(2):
                    S.add("pool", lambda e, hlf=hlf: e.dma_start(out=PA[:, hlf * 4:(hlf + 1) * 4, :],
                                                                in_=W["w_proj_da"][hlf * 512:(hlf + 1) * 512, :].rearrange("(k p) f -> p k f", p=128)),
                          w=[("PA", hlf)], dma="cw0")
                    S.add("pool", lambda e, hlf=hlf: e.dma_start(out=PB[:, hlf * 4:(hlf + 1) * 4, :],
                                                                in_=W["w_proj_swa"][hlf * 512:(hlf + 1) * 512, :].rearrange("(k p) f -> p k f", p=128)),
                          w=[("PB", hlf)], dma="cw0")
                for hlf in range(2):
                    S.add("pool", lambda e, hlf=hlf: e.dma_start(out=WO[:, hlf * 4:(hlf + 1) * 4, :],
                                                                in_=W["w_out"][hlf * 512:(hlf + 1) * 512, :].rearrange("(k p) f -> p k f", p=128)),
                          w=[("WO", hlf)], dma="cw1")

                def load_g(c):
                    s = c % 2
                    S.add("sp", lambda e: e.dma_start(out=ga[s], in_=gTs[c * 128:(c + 1) * 128, r0:r0 + TB]), w=[("ga", s)], dma="cga%d" % s)
                    S.add("sp", lambda e: e.dma_start(out=gbt[s], in_=gTs[1024 + c * 128:1024 + (c + 1) * 128, r0:r0 + TB]), w=[("gbt", s)], dma="cgb%d" % s)
                load_g(0)
                load_g(1)
                for t in range(NT):
                    S.add("sp", lambda e, t=t: e.dma_start(out=xres[:, t, :], in_=x1s[r0 + t * 128:r0 + (t + 1) * 128, :]),
                          w=[("x", t)], dma="x%d" % t)
                cnt = 0
                for c in range(8):
                    s = c % 2
                    for sub in range(TB // 512):
                        par = cnt % 2
                        cnt += 1
                        for k in range(8):
                            S.add("pe", lambda e, k=k, c=c, sub=sub, par=par: e.matmul(
                                pa[par], lhsT=PA[:, k, c * 128:(c + 1) * 128], rhs=oTa[:, k, sub * 512:(sub + 1) * 512],
                                start=(k == 0), stop=(k == 7)), r=[("PA", k // 4), "oTa"], w=[("pa", par)])
                        for k in range(8):
                            S.add("pe", lambda e, k=k, c=c, sub=sub, par=par: e.matmul(
                                pb[par], lhsT=PB[:, k, c * 128:(c + 1) * 128], rhs=oTb[:, k, sub * 512:(sub + 1) * 512],
                                start=(k == 0), stop=(k == 7)), r=[("PB", k // 4), "oTb"], w=[("pb", par)])
                        S.add("dve", lambda e, par=par, s=s, sub=sub: e.tensor_tensor(
                            out=m1[par], in0=pa[par], in1=ga[s][:, sub * 512:(sub + 1) * 512], op=ALU.mult),
                            r=[("pa", par), ("ga", s)], w=[("m1", par)])
                        S.add("dve", lambda e, par=par, s=s, sub=sub: e.tensor_tensor(
                            out=m2[par], in0=pb[par], in1=gbt[s][:, sub * 512:(sub + 1) * 512], op=ALU.mult),
                            r=[("pb", par), ("gbt", s)], w=[("m2", par)])
                        S.add("pool", lambda e, par=par, c=c, sub=sub: e.tensor_tensor(
                            out=mT[:, c, sub * 512:(sub + 1) * 512], in0=m1[par], in1=m2[par], op=ALU.add),
                            r=[("m1", par), ("m2", par)], w=[("mT", c, sub)])
                    if c + 2 < 8:
                        load_g(c + 2)
                cnt = 0
                for t in range(NT):
                    for half in range(2):
                        par = cnt % 2
                        cnt += 1
                        for c in range(8):
                            S.add("pe", lambda e, t=t, half=half, c=c, par=par: e.matmul(
                                py[par], lhsT=mT[:, c, t * 128:(t + 1) * 128], rhs=WO[:, c, half * 512:(half + 1) * 512],
                                start=(c == 0), stop=(c == 7)), r=[("mT", c, t // 4), ("WO", c // 4)], w=[("py", par)])
                        S.add("dve", lambda e, t=t, half=half, par=par: e.tensor_tensor(
                            out=xres[:, t, half * 512:(half + 1) * 512], in0=py[par], in1=xres[:, t, half * 512:(half + 1) * 512], op=ALU.add),
                            r=[("py", par), ("x", t)], w=[("x", t)])
                norm_to_hT(xres, 2, hT, hb, ptr, junk, ssq, lnv, rstd)
                S.barrier()
            with ExitStack() as _st13:
                pg0 = _st13.enter_context(pst("pg0", [128, 512], F32))
                pg1 = _st13.enter_context(pst("pg1", [128, 512], F32))
                pu0 = _st13.enter_context(pst("pu0", [128, 512], F32))
                pu1 = _st13.enter_context(pst("pu1", [128, 512], F32))
                py0 = _st13.enter_context(pst("py0", [128, 512], F32))
                py1 = _st13.enter_context(pst("py1", [128, 512], F32))
                ofl = [_st13.enter_context(sb("of%d" % i_, [128, D], F32)) for i_ in range(3)]
                pg = [pg0.ap(), pg1.ap()]
                pu = [pu0.ap(), pu1.ap()]
                py = [py0.ap(), py1.ap()]
                ofb = [t_.ap() for t_ in ofl]
                ffn(xres, hT, W["ffn2_gate"], W["ffn2_up"], W["ffn2_down"], pg, pu, py)
                def fstats(t):
                    S.add("act", lambda e: e.activation(out=junk, in_=xres[:, t, :], func=AF.Square, scale=1.0 / 32.0,
                                                        accum_out=ssq[:, t:t + 1]), r=[("x", t)], w=["junk", ("ssq", t)])
                    S.add("act", lambda e: e.activation(out=lnv[:, t:t + 1], in_=ssq[:, t:t + 1], func=AF.Ln, bias=epsb, scale=1.0),
                          r=[("ssq", t)], w=[("lnv", t)])
                    S.add("act", lambda e: e.activation(out=rstd[:, t:t + 1], in_=lnv[:, t:t + 1], func=AF.Exp, scale=-0.5),
                          r=[("lnv", t)], w=[("rstd", t)])
                for t in range(NT):
                    fstats(t)
                    S.add("dve", lambda e, t=t: e.scalar_tensor_tensor(out=ofb[t % 3], in0=xres[:, t, :], scalar=rstd[:, t:t + 1],
                                                                     in1=gb[:, 3, :], op0=ALU.mult, op1=ALU.mult),
                          r=[("x", t), ("rstd", t), ("gb", 3)], w=[("ofb", t % 3)])
                    S.add("sp", lambda e, t=t: e.dma_start(out=out[r0 + t * 128:r0 + (t + 1) * 128, :], in_=ofb[t % 3]),
                          r=[("ofb", t % 3)], dma="sto%d" % (t % 3))
                S.barrier()

    setup()
    for seq in range(nseq):
        if "A" in phases:
            phase_A(2 * seq)
            phase_A(2 * seq + 1)
        if "B" in phases:
            with ExitStack() as _stb:
                otok1 = _stb.enter_context(sb("otok", [128, 16, D], BF16)).ap()
                phase_B1(seq, otok1)
                phase_B2(seq, pre=lambda: store_oT(otok1, oTd, seq * SEQ, "d", barrier=False))
        else:
            if "B1" in phases:
                with ExitStack() as _stb:
                    otok1 = _stb.enter_context(sb("otok", [128, 16, D], BF16)).ap()
                    phase_B1(seq, otok1)
                    store_oT(otok1, oTd, seq * SEQ, "d")
            if "B2" in phases:
                phase_B2(seq)
        if "C" in phases:
            phase_C(2 * seq)
            phase_C(2 * seq + 1)
    S.finish()
    return nc


_WNAMES = ["norm_ffn1", "ffn1_gate", "ffn1_up", "ffn1_down", "norm_mix", "w_in", "b_gate", "da_lambda",
           "da_subnorm", "swa_sink", "w_proj_da", "w_proj_swa", "w_out", "norm_ffn2", "ffn2_gate", "ffn2_up",
           "ffn2_down", "norm_final"]


def prep_weights(inputs):
    w = {}
    for nm in _WNAMES:
        a = np.asarray(inputs[nm], dtype=np.float32)
        if nm != "norm_final":
            a = a[0]
        if nm in ("b_gate", "da_lambda"):
            a = a.reshape(-1)
        w[nm] = np.ascontiguousarray(a)
    return w


def kernel(**inputs):
    x = np.asarray(inputs["x"], dtype=np.float32)
    w = prep_weights(inputs)
    nc = build_program()
    in_maps = []
    for c in range(NCORES):
        m = dict(w)
        m["x"] = np.ascontiguousarray(x[2 * c:2 * c + 2].reshape(T, D))
        in_maps.append(m)
    res = run_bass_kernel_spmd(nc, in_maps, core_ids=list(range(NCORES)))
    outs = [np.asarray(r["out"], dtype=np.float32).reshape(2, SEQ, D) for r in res.results]
    return np.concatenate(outs, axis=0)
```
